# Optimizing a Trainium2 kernel written in Bass

```python
import jax, jax.numpy as jnp
from jax import lax
import numpy as np

D_MODEL = 1024
BATCH = 8
SEQ = 2048
DEPTH = 4
DEC_BATCH = 128
DEC_SEQ = 8
PAST_LEN = 16384
PAGE_SIZE = 128

N_MIXERS = 2
EXPAND = 2
D_INNER = EXPAND * D_MODEL
HEAD_DIM = 64
N_HEADS = D_INNER // HEAD_DIM
W_LORA = 96
A_LORA = 96
V_LORA = 64
POOL_WINDOWS = (2, 4, 8, 16)
N_POOL_GROUPS = len(POOL_WINDOWS)
POOL_GROUP = D_INNER // N_POOL_GROUPS
POOL_BUF = max(POOL_WINDOWS) - 1
N_RWKV = (DEPTH + 1) // 2
N_POOL = DEPTH // 2
NORM_EPS = 1e-6
GN_EPS = 64e-5

kernel_name = 'rwkv7_pool_hybrid_step'


def rms_norm(x, g):
    xf = x.astype(jnp.float32)
    y = xf * lax.rsqrt(jnp.mean(jnp.square(xf), axis=-1, keepdims=True) + NORM_EPS)
    return y.astype(x.dtype) * g


def wkv7_scan(S0, r, w, k, v, a, b):
    def step(S, inp):
        r_t, w_t, k_t, v_t, a_t, b_t = inp
        Sa = jnp.einsum('bhij,bhj->bhi', S, a_t)
        S = S * w_t[:, :, None, :] + Sa[..., None] * b_t[:, :, None, :] + v_t[..., None] * k_t[:, :, None, :]
        return S, jnp.einsum('bhij,bhj->bhi', S, r_t)
    xs = (jnp.moveaxis(r, 1, 0), jnp.moveaxis(w, 1, 0), jnp.moveaxis(k, 1, 0),
          jnp.moveaxis(v, 1, 0), jnp.moveaxis(a, 1, 0), jnp.moveaxis(b, 1, 0))
    S, ys = lax.scan(step, S0, xs)
    return jnp.moveaxis(ys, 0, 1), S


def rwkv7_branch(xn, shift_prev, wkv_prev, v_first, vres, mu, w_r, w_k, w_v, w_z,
                 w0, w1, w2, a0, a1, a2, k_k, k_a, r_k, gn_g, gn_b, w_o):
    B, T, _ = xn.shape
    f32 = jnp.float32
    x_prev = jnp.concatenate([shift_prev[:, None, :].astype(xn.dtype), xn[:, :-1]], axis=1)
    dx = x_prev - xn
    xr, xw, xk, xv, xa, xg = (xn + dx * mu[m] for m in range(6))
    r = xr @ w_r
    k = xk @ w_k
    v = xv @ w_v
    z = xg @ w_z
    w_log = -jax.nn.softplus(-(w0 + jnp.tanh(xw @ w1) @ w2).astype(f32)) - 0.5
    decay = jnp.exp(-jnp.exp(w_log))
    a = jax.nn.sigmoid((a0 + (xa @ a1) @ a2).astype(f32))
    if vres is None:
        v_first = v
    else:
        v0, v1, v2 = vres
        v = v + (v_first - v) * jax.nn.sigmoid(v0 + (xv @ v1) @ v2)
    shp = (B, T, N_HEADS, HEAD_DIM)
    r_h = r.astype(f32).reshape(shp)
    v_h = v.astype(f32).reshape(shp)
    a_h = a.reshape(shp)
    w_h = decay.reshape(shp)
    kk = (k * k_k).astype(f32).reshape(shp)
    kk = kk / jnp.maximum(jnp.sqrt(jnp.sum(jnp.square(kk), axis=-1, keepdims=True)), 1e-12)
    k_h = k.astype(f32).reshape(shp) * (1.0 + (a_h - 1.0) * k_a.astype(f32).reshape(N_HEADS, HEAD_DIM))
    y, S = wkv7_scan(wkv_prev.astype(f32), r_h, w_h, k_h, v_h, -kk, kk * a_h)
    mean = jnp.mean(y, axis=-1, keepdims=True)
    var = jnp.mean(jnp.square(y - mean), axis=-1, keepdims=True)
    y = ((y - mean) * lax.rsqrt(var + GN_EPS)).reshape(B, T, D_INNER) * gn_g + gn_b
    bonus = jnp.sum(r_h * k_h * r_k.astype(f32), axis=-1, keepdims=True) * v_h
    y = y + bonus.reshape(B, T, D_INNER)
    out = (y.astype(xn.dtype) * jax.nn.silu(z)) @ w_o
    return out, v_first, xn[:, -1], S.astype(wkv_prev.dtype)


def pool_branch(xn, buf_prev, pos0, w_in, w_grp, b_grp, scale, w_o):
    B, T, _ = xn.shape
    f32 = jnp.float32
    uz = xn @ w_in
    u, z = uz[..., :D_INNER], uz[..., D_INNER:]
    ext = jnp.concatenate([buf_prev.astype(u.dtype), u], axis=1)
    cs = jnp.cumsum(ext.astype(f32), axis=1)
    cs = jnp.concatenate([jnp.zeros((B, 1, D_INNER), f32), cs], axis=1)
    end = cs[:, POOL_BUF + 1:]
    pos = pos0 + jnp.arange(T, dtype=jnp.int32)
    pooled = []
    for g, win in enumerate(POOL_WINDOWS):
        lo, hi = g * POOL_GROUP, (g + 1) * POOL_GROUP
        start = cs[:, POOL_BUF + 1 - win:POOL_BUF + 1 - win + T, lo:hi]
        cnt = jnp.minimum(win, pos + 1).astype(f32)[None, :, None]
        pooled.append((end[..., lo:hi] - start) / cnt)
    p = (jnp.concatenate(pooled, axis=-1) - u.astype(f32)).astype(u.dtype)
    p = p.reshape(B, T, N_POOL_GROUPS, POOL_GROUP)
    mixed = jnp.einsum('btgi,gio->btgo', p, w_grp) + b_grp
    mixed = mixed.reshape(B, T, D_INNER) * scale
    out = (mixed * jax.nn.silu(z)) @ w_o
    return out, ext[:, -POOL_BUF:]


def trunk(x, shift0, wkv0, buf0, pos0, p):
    new_shift, new_wkv, new_buf = [], [], []
    v_first = None
    for i in range(DEPTH):
        j = i // N_MIXERS
        if i % N_MIXERS == 0:
            xn = rms_norm(x, p['rwkv_norm'][j])
            vres = None if j == 0 else (p['rwkv_v0'][j - 1], p['rwkv_v1'][j - 1], p['rwkv_v2'][j - 1])
            out, v_first, sh, S = rwkv7_branch(
                xn, shift0[j], wkv0[j], v_first, vres, p['rwkv_mu'][j],
                p['rwkv_w_r'][j], p['rwkv_w_k'][j], p['rwkv_w_v'][j], p['rwkv_w_z'][j],
                p['rwkv_w0'][j], p['rwkv_w1'][j], p['rwkv_w2'][j],
                p['rwkv_a0'][j], p['rwkv_a1'][j], p['rwkv_a2'][j],
                p['rwkv_k_k'][j], p['rwkv_k_a'][j], p['rwkv_r_k'][j],
                p['rwkv_gn_g'][j], p['rwkv_gn_b'][j], p['rwkv_w_o'][j])
            new_shift.append(sh)
            new_wkv.append(S)
        else:
            xn = rms_norm(x, p['pool_norm'][j])
            out, buf = pool_branch(xn, buf0[j], pos0, p['pool_w_in'][j], p['pool_w_grp'][j],
                                   p['pool_b_grp'][j], p['pool_scale'][j], p['pool_w_o'][j])
            new_buf.append(buf)
        x = x + out
    return rms_norm(x, p['final_norm']), jnp.stack(new_shift), jnp.stack(new_wkv), jnp.stack(new_buf)


def setup_inputs(seed: int = 0) -> dict:
    key = jax.random.key(seed)
    ks = iter(jax.random.split(key, 64))
    f32 = jnp.float32
    def nrm(shape, s):
        return jax.random.normal(next(ks), shape, f32) * s
    def uni(shape, lo, hi):
        return jax.random.uniform(next(ks), shape, f32, lo, hi)
    D, C, NR, NP = D_MODEL, D_INNER, N_RWKV, N_POOL
    NV = max(NR - 1, 0)
    return {
        'x_prompt': nrm((BATCH, SEQ, D), 1.0),
        'x_sample': nrm((DEC_BATCH, DEC_SEQ, D), 1.0),
        'state_shift': nrm((NR, DEC_BATCH, D), 1.0),
        'state_wkv': nrm((NR, DEC_BATCH, N_HEADS, HEAD_DIM, HEAD_DIM), 0.5),
        'state_pool': nrm((NP, DEC_BATCH, POOL_BUF, C), 1.0),
        'rwkv_norm': 1.0 + nrm((NR, D), 0.05),
        'rwkv_mu': uni((NR, 6, D), 0.0, 1.0),
        'rwkv_w_r': nrm((NR, D, C), D ** -0.5),
        'rwkv_w_k': nrm((NR, D, C), D ** -0.5),
        'rwkv_w_v': nrm((NR, D, C), D ** -0.5),
        'rwkv_w_z': nrm((NR, D, C), D ** -0.5),
        'rwkv_w0': uni((NR, C), -6.0, -1.0),
        'rwkv_w1': nrm((NR, D, W_LORA), D ** -0.5),
        'rwkv_w2': nrm((NR, W_LORA, C), 0.3 * W_LORA ** -0.5),
        'rwkv_a0': nrm((NR, C), 0.1),
        'rwkv_a1': nrm((NR, D, A_LORA), D ** -0.5),
        'rwkv_a2': nrm((NR, A_LORA, C), 0.5 * A_LORA ** -0.5),
        'rwkv_v0': nrm((NV, C), 0.1),
        'rwkv_v1': nrm((NV, D, V_LORA), D ** -0.5),
        'rwkv_v2': nrm((NV, V_LORA, C), 0.5 * V_LORA ** -0.5),
        'rwkv_k_k': 0.85 + nrm((NR, C), 0.05),
        'rwkv_k_a': 1.0 + nrm((NR, C), 0.05),
        'rwkv_r_k': nrm((NR, N_HEADS, HEAD_DIM), 0.1),
        'rwkv_gn_g': 1.0 + nrm((NR, C), 0.05),
        'rwkv_gn_b': nrm((NR, C), 0.02),
        'rwkv_w_o': nrm((NR, C, D), C ** -0.5),
        'pool_norm': 1.0 + nrm((NP, D), 0.05),
        'pool_w_in': nrm((NP, D, 2 * C), D ** -0.5),
        'pool_w_grp': nrm((NP, N_POOL_GROUPS, POOL_GROUP, POOL_GROUP), POOL_GROUP ** -0.5),
        'pool_b_grp': nrm((NP, N_POOL_GROUPS, POOL_GROUP), 0.02),
        'pool_scale': 1.0 + nrm((NP, C), 0.1),
        'pool_w_o': nrm((NP, C, D), C ** -0.5),
        'final_norm': 1.0 + nrm((D,), 0.05),
    }


def reference(x_prompt, x_sample, state_shift, state_wkv, state_pool,
              rwkv_norm, rwkv_mu, rwkv_w_r, rwkv_w_k, rwkv_w_v, rwkv_w_z,
              rwkv_w0, rwkv_w1, rwkv_w2, rwkv_a0, rwkv_a1, rwkv_a2,
              rwkv_v0, rwkv_v1, rwkv_v2, rwkv_k_k, rwkv_k_a, rwkv_r_k,
              rwkv_gn_g, rwkv_gn_b, rwkv_w_o,
              pool_norm, pool_w_in, pool_w_grp, pool_b_grp, pool_scale, pool_w_o,
              final_norm):
    p = dict(rwkv_norm=rwkv_norm, rwkv_mu=rwkv_mu, rwkv_w_r=rwkv_w_r, rwkv_w_k=rwkv_w_k,
             rwkv_w_v=rwkv_w_v, rwkv_w_z=rwkv_w_z, rwkv_w0=rwkv_w0, rwkv_w1=rwkv_w1,
             rwkv_w2=rwkv_w2, rwkv_a0=rwkv_a0, rwkv_a1=rwkv_a1, rwkv_a2=rwkv_a2,
             rwkv_v0=rwkv_v0, rwkv_v1=rwkv_v1, rwkv_v2=rwkv_v2, rwkv_k_k=rwkv_k_k,
             rwkv_k_a=rwkv_k_a, rwkv_r_k=rwkv_r_k, rwkv_gn_g=rwkv_gn_g, rwkv_gn_b=rwkv_gn_b,
             rwkv_w_o=rwkv_w_o, pool_norm=pool_norm, pool_w_in=pool_w_in,
             pool_w_grp=pool_w_grp, pool_b_grp=pool_b_grp, pool_scale=pool_scale,
             pool_w_o=pool_w_o, final_norm=final_norm)
    B = x_prompt.shape[0]
    shift0 = jnp.zeros((N_RWKV, B, D_MODEL), state_shift.dtype)
    wkv0 = jnp.zeros((N_RWKV, B, N_HEADS, HEAD_DIM, HEAD_DIM), state_wkv.dtype)
    buf0 = jnp.zeros((N_POOL, B, POOL_BUF, D_INNER), state_pool.dtype)
    y_prompt, sh_p, wkv_p, pool_p = trunk(x_prompt, shift0, wkv0, buf0, 0, p)
    y_sample, sh_s, wkv_s, pool_s = trunk(x_sample, state_shift, state_wkv, state_pool, PAST_LEN, p)
    sh_p = sh_p.astype(state_shift.dtype)
    pool_p = pool_p.astype(state_pool.dtype)
    sh_s = sh_s.astype(state_shift.dtype)
    pool_s = pool_s.astype(state_pool.dtype)
    return (y_prompt, y_sample, sh_p, wkv_p, pool_p, sh_s, wkv_s, pool_s)
```

```python
import numpy as np
import concourse.bass as bass
import concourse.mybir as mybir
from concourse.bass_utils import run_bass_kernel_spmd
from contextlib import ExitStack

F32 = mybir.dt.float32
BF16 = mybir.dt.bfloat16
AF = mybir.ActivationFunctionType
ALU = mybir.AluOpType
AX = mybir.AxisListType

NCORES = 8
D = 1024
C = 2048
NH = 32
HD = 64
SEQ = 2048
TS = 8
NSEQ = 16
NCH = D // 128
NHP = C // 128
W_LORA = 96
A_LORA = 96
V_LORA = 64
PBUF = 15
NSUB = 2
TT = 128 * NSUB
NMT = SEQ // TT
C0 = float(np.exp(-0.5))
NORM_EPS = 1e-6
GN_EPS = 64e-5
PAST_LEN = 16384
WINS = (2, 4, 8, 16)
SAME_ENG_SYNC = True
OVERLAP_AB = True
TPHASE_INTERLEAVE = False
AB_RATIO = 2
NO_SELF_SYNC = ("pe",)
NG = C // 256


class Res:
    __slots__ = ("w", "r", "excl")

    def __init__(self, excl=False):
        self.w = None
        self.r = {}
        self.excl = excl


class V:
    __slots__ = ("ap", "res")

    def __init__(self, ap, res):
        self.ap = ap
        self.res = res

    def re(self, pat, **kw):
        return V(self.ap.rearrange(pat, **kw), self.res)

    def bc(self, shape):
        return V(self.ap.to_broadcast(list(shape)), self.res)

    def __getitem__(self, k):
        return V(self.ap[k], self.res)


class Tile:
    def __init__(self, handle, res=None, is_ap=False):
        self.h = handle
        self.res = res or Res()
        self.is_ap = is_ap

    def __getitem__(self, k):
        return V(self.h[k], self.res)

    def all(self):
        return V(self.h if self.is_ap else self.h[:], self.res)


class KB:
    def __init__(self, nc):
        self.nc = nc
        self.es = ExitStack()
        self.E = dict(pe=nc.tensor, act=nc.scalar, dve=nc.vector, pool=nc.gpsimd, sp=nc.sync)
        self.cnt = {}
        self.semi = {e: 0 for e in self.E}
        self.cursem = {}
        for e in self.E:
            self._newsem(e)
        self.seen = {e: {} for e in self.E}
        self.nd = 12
        self.dsems = {q: [self.es.enter_context(nc.semaphore(f"d{q}{i}")) for i in range(self.nd)] for q in ("sp", "pool")}
        self.dcnt = {q: [0] * self.nd for q in ("sp", "pool")}
        self.dnext = {q: 0 for q in ("sp", "pool")}
        self.ntile = 0
        self.out_toks = []

    def _newsem(self, e):
        nm = f"s_{e}_{self.semi[e]}"
        s = self.es.enter_context(self.nc.semaphore(nm))
        self.semi[e] += 1
        self.cursem[e] = (s, nm)
        self.cnt[e] = 0

    def sb(self, shape, dt=F32, name=None):
        self.ntile += 1
        nm = name or f"t{self.ntile}"
        return Tile(self.es.enter_context(self.nc.sbuf_tensor(f"{nm}_{self.ntile}", list(shape), dt)))

    @staticmethod
    def view(ap, res):
        return Tile(ap, res=res, is_ap=True)

    def ps(self, name):
        return Tile(self.es.enter_context(self.nc.psum_tensor(name, [128, 512], F32)), res=Res(excl=True))

    def dram(self, name, shape, dt, kind):
        return Tile(self.nc.dram_tensor(name, list(shape), dt, kind=kind).ap(), is_ap=True)

    def _wait(self, e, tok):
        sem, key, val, te = tok
        if te == e and (e in NO_SELF_SYNC or not SAME_ENG_SYNC):
            return
        if self.seen[e].get(key, 0) >= val:
            return
        self.E[e].wait_ge(sem, val)
        self.seen[e][key] = val

    def _deps(self, e, reads, writes):
        for r in reads:
            if r.w is not None:
                self._wait(e, r.w)
        for w in writes:
            if w.w is not None:
                self._wait(e, w.w)
            for tok in w.r.values():
                self._wait(e, tok)

    def _mark(self, tok, reads, writes):
        for r in reads:
            r.r[tok[1]] = tok
        for w in writes:
            w.w = tok
            w.r = {}

    @staticmethod
    def _rl(xs):
        out = []
        for x in xs:
            if x is None or isinstance(x, (int, float)):
                continue
            r = x if isinstance(x, Res) else x.res
            if isinstance(r, (tuple, list)):
                out.extend(r)
            else:
                out.append(r)
        return out

    def op(self, e, fn, reads, writes):
        reads = self._rl(reads)
        writes = self._rl(writes)
        writes = writes + [r for r in reads if r.excl]
        reads = [r for r in reads if not r.excl]
        self._deps(e, reads, writes)
        inst = fn(self.E[e])
        if self.cnt[e] >= 30000:
            self._newsem(e)
        self.cnt[e] += 1
        sem, nm = self.cursem[e]
        inst.then_inc(sem, 1)
        tok = (sem, nm, self.cnt[e], e)
        self._mark(tok, reads, writes)
        return tok

    def dma(self, q, out, in_, extra_reads=(), is_out=False):
        reads = self._rl([in_] + list(extra_reads))
        writes = self._rl([out])
        k = self.dnext[q]
        self.dnext[q] = (k + 1) % self.nd
        sem = self.dsems[q][k]
        if self.dcnt[q][k] > 0:
            self._wait(q, (sem, f"d{q}{k}", self.dcnt[q][k] * 16, None))
        self._deps(q, reads, writes)
        inst = self.E[q].dma_start(out=out.ap, in_=in_.ap)
        self.dcnt[q][k] += 1
        inst.then_inc(sem, 16)
        tok = (sem, f"d{q}{k}", self.dcnt[q][k] * 16, None)
        self._mark(tok, reads, writes)
        if is_out:
            self.out_toks.append(tok)
        return tok

    @staticmethod
    def _a(x):
        return x.ap if isinstance(x, V) else x

    def act(self, out, in_, func, bias=None, scale=None, accum=None):
        kw = {}
        if bias is not None:
            kw["bias"] = self._a(bias)
        if scale is not None:
            kw["scale"] = self._a(scale)
        if accum is not None:
            kw["accum_out"] = accum.ap
        rd = [in_] + [x for x in (bias, scale) if isinstance(x, V)]
        wr = [out] + ([accum] if accum is not None else [])
        return self.op("act", lambda e: e.activation(out=out.ap, in_=in_.ap, func=func, **kw), rd, wr)

    def tt(self, out, in0, in1, op, eng="dve"):
        return self.op(eng, lambda e: e.tensor_tensor(out=out.ap, in0=in0.ap, in1=in1.ap, op=op), [in0, in1], [out])

    def ts(self, out, in0, s1, op0, s2=None, op1=None, eng="dve"):
        rd = [in0] + [x for x in (s1, s2) if isinstance(x, V)]
        if op1 is None:
            return self.op(eng, lambda e: e.tensor_scalar(out=out.ap, in0=in0.ap, scalar1=self._a(s1), scalar2=None, op0=op0), rd, [out])
        return self.op(eng, lambda e: e.tensor_scalar(out=out.ap, in0=in0.ap, scalar1=self._a(s1), scalar2=self._a(s2), op0=op0, op1=op1), rd, [out])

    def stt(self, out, in0, scalar, in1, op0, op1):
        rd = [in0, in1] + ([scalar] if isinstance(scalar, V) else [])
        return self.op("dve", lambda e: e.scalar_tensor_tensor(out=out.ap, in0=in0.ap, scalar=self._a(scalar), in1=in1.ap, op0=op0, op1=op1), rd, [out])

    def cp(self, out, in_, eng="dve"):
        if eng == "act":
            return self.act(out, in_, AF.Copy)
        return self.op(eng, lambda e: e.tensor_copy(out=out.ap, in_=in_.ap), [in_], [out])

    def pe(self, fns, reads, writes):
        def run(e):
            inst = None
            for f in fns:
                inst = f(e)
            return inst
        return self.op("pe", run, reads, writes)

    def mm(self, out, lhsT, rhs, start=True, stop=True):
        return lambda e: e.matmul(out.ap, lhsT=lhsT.ap, rhs=rhs.ap, start=start, stop=stop, skip_group_check=True)

    def tr(self, out, in_, ident):
        return lambda e: e.transpose(out.ap, in_.ap, ident.ap)


def _consts():
    c = {}
    s = np.arange(128)[:, None]
    t = np.arange(128)[None, :]
    same8 = (s // 8) == (t // 8)
    c["ident"] = np.eye(128, dtype=np.float32)
    c["blk2"] = ((s // 64) == (t // 64)).astype(np.float32)
    c["ms_p"] = (s < t).astype(np.float32)
    c["mi_p"] = (s <= t).astype(np.float32)
    c["mst_p"] = (t < s).astype(np.float32)
    c["ms_s"] = ((s < t) & same8).astype(np.float32)
    c["mi_s"] = ((s <= t) & same8).astype(np.float32)
    c["mst_s"] = ((t < s) & same8).astype(np.float32)
    nfp = np.ones((128, TT), np.float32)
    nfp[:, ::128] = 0.0
    nfs = np.ones((128, TT), np.float32)
    nfs[:, ::8] = 0.0
    c["nf_p"] = nfp
    c["nf_s"] = nfs
    sel = np.zeros((128, NSEQ, 128), np.float32)
    for b in range(NSEQ):
        sel[:, b, b * 8:(b + 1) * 8] = 1.0
    c["sel"] = sel
    selt = np.zeros((128, NSEQ), np.float32)
    for b in range(NSEQ):
        selt[b * 8:(b + 1) * 8, b] = 1.0
    c["selt"] = selt
    bc = np.zeros((4, 128, 128), np.float32)
    bf = np.zeros((4, 128, 128), np.float32)
    bp = np.zeros((4, 128, 128), np.float32)
    bsc = np.zeros((4, 128, 128), np.float32)
    bsp = np.zeros((4, 2, 128, 128), np.float32)
    for g, win in enumerate(WINS):
        inw = (s > t - win) & (s <= t)
        bc[g] = inw / win - np.eye(128)
        cnt = np.minimum(win, t + 1)
        bf[g] = inw / cnt - np.eye(128)
        bp[g] = ((s - 128) > (t - win)) / win
        bsc[g] = (inw & same8) / win - np.eye(128)
        for half in range(2):
            for b8 in range(8):
                b = half * 8 + b8
                for r in range(PBUF):
                    for tt in range(TS):
                        if r > PBUF + tt - win:
                            bsp[g, half, b8 * PBUF + r, b * 8 + tt] = 1.0 / win
    c["bc"] = np.ascontiguousarray(bc.transpose(1, 0, 2))
    c["bf"] = np.ascontiguousarray(bf.transpose(1, 0, 2))
    bpc = np.zeros((64, 4, 128), np.float32)
    for g in range(4):
        for r in range(16):
            bpc[32 * (g % 2) + r, g, :] = bp[g, 112 + r, :]
    c["bpc"] = bpc
    c["bprev64"] = np.ascontiguousarray(bp.transpose(1, 0, 2))
    c["bsc"] = np.ascontiguousarray(bsc.transpose(1, 0, 2))
    c["bsp"] = np.ascontiguousarray(bsp.transpose(2, 0, 1, 3)).reshape(128, 8, 128)
    return c


CONST_SHAPES = {k: v.shape for k, v in _consts().items()}

VEC_LAYOUT = [
    ("rnorm", 2 * 8), ("mu", 2 * 6 * 8), ("w0", 32), ("a0", 32), ("k_k", 32), ("k_a", 32), ("r_k", 32),
    ("gn_g", 32), ("gn_b", 32), ("v0", 16), ("pnorm", 16), ("pb", 32), ("pscale", 32),
]
VEC_OFF = {}
_o = 0
for _n, _w in VEC_LAYOUT:
    VEC_OFF[_n] = _o
    _o += _w
NVEC = _o


def _fm(a, n):
    a = np.asarray(a, np.float32)
    lead = a.shape[:-1]
    a = a.reshape(lead + (n, 128))
    a = np.moveaxis(a, -1, 0)
    return a.reshape(128, -1)


def _pack_vecs(inp):
    cols = [
        _fm(inp["rwkv_norm"], 8), _fm(inp["rwkv_mu"], 8), _fm(inp["rwkv_w0"], 16), _fm(inp["rwkv_a0"], 16),
        _fm(inp["rwkv_k_k"], 16), _fm(inp["rwkv_k_a"], 16), _fm(inp["rwkv_r_k"].reshape(2, C), 16),
        _fm(inp["rwkv_gn_g"], 16), _fm(inp["rwkv_gn_b"], 16), _fm(inp["rwkv_v0"], 16),
        _fm(inp["pool_norm"], 8), _fm(inp["pool_b_grp"].reshape(2, C), 16), _fm(inp["pool_scale"], 16),
    ]
    out = np.ascontiguousarray(np.concatenate(cols, axis=1))
    assert out.shape == (128, NVEC), out.shape
    return out


class StopEmit(Exception):
    pass


def build(dbg=None, stop_after=None, do_sample=True, n_mt=NMT, cut=None):
    nc = bass.Bass("TRN2", target_bir_lowering=False)
    kb = KB(nc)
    dbg = dbg or {}

    def din(name, shape, dt=F32):
        return kb.dram(name, shape, dt, "ExternalInput")

    def dout(name, shape):
        return kb.dram(name, shape, F32, "ExternalOutput")

    xp = din("xp", [SEQ, D])
    xs = din("xs", [128, D])
    sshift = din("sshift", [2, NSEQ, D])
    swkv = din("swkv", [2, NSEQ, NH, HD, HD])
    spool = din("spool", [2, NSEQ, PBUF, C])
    vecs_d = din("vecs", [128, NVEC])
    fnorm_d = din("final_norm", [D])
    rnorm_d = din("rwkv_norm", [2, D])
    cd = {k: din("c_" + k, list(shp)) for k, shp in CONST_SHAPES.items()}
    W = {}
    for nm, shp in [("rwkv_w_r", [2, D, C]), ("rwkv_w_k", [2, D, C]), ("rwkv_w_v", [2, D, C]), ("rwkv_w_z", [2, D, C]),
                    ("rwkv_w1", [2, D, W_LORA]), ("rwkv_w2", [2, W_LORA, C]), ("rwkv_a1", [2, D, A_LORA]),
                    ("rwkv_a2", [2, A_LORA, C]), ("rwkv_v1", [1, D, V_LORA]), ("rwkv_v2", [1, V_LORA, C]),
                    ("rwkv_w_o", [2, C, D]), ("pool_w_in", [2, D, 2 * C]), ("pool_w_grp", [2, 4, 512, 512]),
                    ("pool_w_o", [2, C, D])]:
        W[nm] = din(nm, shp)

    y_p = dout("y_p", [SEQ, D])
    y_s = dout("y_s", [128, D])
    shp_o = dout("shp", [2, D])
    wkvp_o = dout("wkvp", [2, NH, HD, HD])
    poolp_o = dout("poolp", [2, PBUF, C])
    shs_o = dout("shs", [2, NSEQ, D])
    wkvs_o = dout("wkvs", [2, NSEQ, NH, HD, HD])
    pools_u = dout("pools_u", [2, 128, C])
    pools_old = dout("pools_old", [2, NSEQ, 7, C])
    dbg_o = {k: dout("dbg_" + k, shp) for k, shp in dbg.items()}

    PID = {}
    npiece = 0

    def newp(key):
        nonlocal npiece
        PID[key] = npiece
        npiece += 1

    for j in range(2):
        for g in range(NG):
            for nm in ("r", "k", "v", "z"):
                newp(("rw", j, nm, g))
            newp(("l2", j, g))
        newp(("l1", j))
        for q in range(4):
            for hf in range(2):
                newp(("rwo", j, q, hf))
        for g in range(4):
            for hf in range(2):
                newp(("pu", j, g, hf))
                newp(("pz", j, g, hf))
            newp(("pg", j, g))
        for q in range(4):
            for hf in range(2):
                newp(("pwo", j, q, hf))
    scr = kb.dram("wscr", [npiece, 128, 2048], BF16, "Internal")
    scr_res = [Res() for _ in range(npiece)]

    def scrv(key):
        i = PID[key]
        return V(scr.h[i], scr_res[i])

    def prepass():
        def cast(key, dst_view, src_view):
            i = PID[key]
            kb.dma("pool", V(dst_view, scr_res[i]), src_view)

        for j in range(2):
            for g in range(NG):
                for nm in ("r", "k", "v", "z"):
                    src = W["rwkv_w_" + nm].h[j].rearrange("(c p) n -> p c n", p=128)[:, :, g * 256:(g + 1) * 256]
                    dst = scr.h[PID[("rw", j, nm, g)]].rearrange("p (c n) -> p c n", n=256)
                    cast(("rw", j, nm, g), dst, V(src, W["rwkv_w_" + nm].res))
                dst = scr.h[PID[("l2", j, g)]].rearrange("p (c n) -> p c n", n=256)
                cast(("l2", j, g), dst[0:W_LORA, 0, :], V(W["rwkv_w2"].h[j][:, g * 256:(g + 1) * 256], W["rwkv_w2"].res))
                cast(("l2", j, g), dst[0:A_LORA, 1, :], V(W["rwkv_a2"].h[j][:, g * 256:(g + 1) * 256], W["rwkv_a2"].res))
                if j == 1:
                    cast(("l2", j, g), dst[0:V_LORA, 2, :], V(W["rwkv_v2"].h[0][:, g * 256:(g + 1) * 256], W["rwkv_v2"].res))
            dst = scr.h[PID[("l1", j)]].rearrange("p (c n) -> p c n", n=256)
            cast(("l1", j), dst[:, :, 0:96], V(W["rwkv_w1"].h[j].rearrange("(c p) n -> p c n", p=128), W["rwkv_w1"].res))
            cast(("l1", j), dst[:, :, 96:192], V(W["rwkv_a1"].h[j].rearrange("(c p) n -> p c n", p=128), W["rwkv_a1"].res))
            if j == 1:
                cast(("l1", j), dst[:, :, 192:256], V(W["rwkv_v1"].h[0].rearrange("(c p) n -> p c n", p=128), W["rwkv_v1"].res))
            for q in range(4):
                for hf in range(2):
                    for (pk, wn) in ((("rwo", j, q, hf), "rwkv_w_o"), (("pwo", j, q, hf), "pool_w_o")):
                        src = W[wn].h[j].rearrange("(c p) n -> p c n", p=128)[:, hf * 8:(hf + 1) * 8, q * 256:(q + 1) * 256]
                        dst = scr.h[PID[pk]].rearrange("p (c n) -> p c n", n=256)
                        cast(pk, dst, V(src, W[wn].res))
            for g in range(4):
                for hf in range(2):
                    c0 = g * 512 + hf * 256
                    src = W["pool_w_in"].h[j].rearrange("(c p) n -> p c n", p=128)
                    dst = scr.h[PID[("pu", j, g, hf)]].rearrange("p (c n) -> p c n", n=256)
                    cast(("pu", j, g, hf), dst, V(src[:, :, c0:c0 + 256], W["pool_w_in"].res))
                    dst = scr.h[PID[("pz", j, g, hf)]].rearrange("p (c n) -> p c n", n=256)
                    cast(("pz", j, g, hf), dst, V(src[:, :, C + c0:C + c0 + 256], W["pool_w_in"].res))
                src = W["pool_w_grp"].h[j, g].rearrange("(c p) n -> p c n", p=128)
                dst = scr.h[PID[("pg", j, g)]].rearrange("p (c n) -> p c n", n=512)
                cast(("pg", j, g), dst, V(src, W["pool_w_grp"].res))

    ident = kb.sb([128, 128], F32, "ident")
    identb = kb.sb([128, 128], BF16, "identb")
    blk2 = kb.sb([128, 128], F32, "blk2")
    msk = {k: kb.sb([128, 128], BF16, k) for k in ("ms_p", "mi_p", "mst_p", "ms_s", "mi_s", "mst_s")}
    msi4 = {k: kb.sb([128, 4, 128], BF16, "msi4" + k) for k in ("_p", "_s")}
    nf = {k: kb.sb([128, TT], F32, k) for k in ("nf_p", "nf_s")}
    selb = kb.sb([128, NSEQ, 128], BF16, "selb")
    seltb = kb.sb([128, NSEQ], BF16, "seltb")
    bands = {k: kb.sb([128, 4, 128], F32, k) for k in ("bc", "bf", "bsc")}
    bpc = kb.sb([64, 4, 128], F32, "bpc")
    bprev_full = kb.sb([128, 4, 128], F32, "bprev")
    bprev64 = kb.view(bprev_full.h[64:128, :, :], bprev_full.res)
    bsp = kb.sb([128, 8, 128], F32, "bsp")
    vecs = kb.sb([128, NVEC], F32, "vecs")
    omka = kb.sb([128, 32], F32, "omka")
    neghalf = kb.sb([128, 8], F32, "neghalf")
    pbs = kb.sb([128, 32], F32, "pbs")
    hbias = kb.sb([128, 80], F32, "hbias")
    stage = kb.sb([128, NSEQ * 128], F32, "stage")

    def vec(name, idx):
        o = VEC_OFF[name] + idx
        return vecs[:, o:o + 1]

    def setup():
        for k in ("ident", "blk2"):
            kb.dma("sp", {"ident": ident, "blk2": blk2}[k].all(), cd[k].all())
        for k in bands:
            kb.dma("sp", bands[k].all(), cd[k].all())
        kb.dma("sp", bsp.all(), cd["bsp"].all())
        kb.dma("sp", bpc.all(), cd["bpc"].all())
        kb.dma("sp", bprev_full.all(), cd["bprev64"].all())
        for k in nf:
            kb.dma("sp", nf[k].all(), cd[k].all())
        kb.dma("sp", vecs.all(), vecs_d.all())
        kb.cp(identb.all(), ident.all())
        for k in msk:
            st = stage[:, 0:128]
            kb.dma("sp", st, cd[k].all())
            kb.cp(msk[k].all(), st)
        for sfx_ in ("_p", "_s"):
            for q_, nm_ in enumerate(("ms", "mi", "ms", "mi")):
                kb.cp(msi4[sfx_][:, q_, :], msk[nm_ + sfx_].all())
        st = stage[:, 0:NSEQ * 128]
        kb.dma("sp", st, cd["sel"].all().re("p b t -> p (b t)"))
        kb.cp(selb.all().re("p b t -> p (b t)"), st)
        st = stage[:, 0:NSEQ]
        kb.dma("sp", st, cd["selt"].all())
        kb.cp(seltb.all(), st)
        for o_, nm_, n_ in ((0, "w0", 32), (32, "a0", 32), (64, "v0", 16)):
            kb.ts(hbias[:, o_:o_ + n_], vecs[:, VEC_OFF[nm_]:VEC_OFF[nm_] + n_], 0.5, ALU.mult)
        kb.op("dve", lambda e: e.memset(neghalf.all().ap, -0.5), [], [neghalf])
        ka = vecs[:, VEC_OFF["k_a"]:VEC_OFF["k_a"] + 32]
        kb.ts(omka.all(), ka, -1.0, ALU.mult, 1.0, ALU.add)
        kb.tt(pbs.all(), vecs[:, VEC_OFF["pb"]:VEC_OFF["pb"] + 32], vecs[:, VEC_OFF["pscale"]:VEC_OFF["pscale"] + 32], ALU.mult)

    x_tok = kb.sb([128, NSUB, D], F32, "x_tok")
    xnT = kb.sb([128, NCH, TT + 1], F32, "xnT")
    ybuf = kb.sb([128, NHP * TT], BF16, "ybuf")
    yzT = kb.view(ybuf.h[:].rearrange("p (c t) -> p c t", t=TT), ybuf.res)
    dx = kb.view(ybuf.h[:].bitcast(F32).rearrange("p (c t) -> p c t", t=TT), ybuf.res)
    xm_ = [kb.sb([128, NCH, TT], BF16, f"xm{i}") for i in range(5)]
    MR, MW, MK, MV, MA, MG = range(6)
    xm = {MR: xm_[0], MK: xm_[1], MV: xm_[2], MG: xm_[3], MW: xm_[4], MA: xm_[4]}
    mids = kb.sb([128, 3, TT], BF16, "mids")
    big = kb.view(stage.h[:, 0:D], stage.res)
    vfirst = kb.sb([128, NHP, TT], BF16, "vfirst")
    shiftc = [kb.sb([128, NCH, 1], F32, f"shiftc{j}") for j in range(2)]
    ST = [kb.sb([128, NHP, HD], F32, f"ST{j}") for j in range(2)]
    STb = [kb.sb([128, NHP, HD], BF16, f"STb{j}") for j in range(2)]
    cc_ = [kb.sb([64, 1024], F32, f"cc{j}") for j in range(2)]
    small = kb.sb([128, 64], F32, "small")

    NSLOT = 5
    wslots = [kb.sb([128, 2048], BF16, f"wslot{i}") for i in range(NSLOT)]
    wnext = [0]

    def wload(key, parts=None):
        s = wslots[wnext[0] % NSLOT]
        wnext[0] += 1
        if parts is None:
            kb.dma("sp", s.all(), scrv(key))
        else:
            src = scrv(key)
            for f in parts:
                kb.dma("sp", V(f(s.h[:]), s.res), V(f(src.ap), src.res))
        return s

    banks = [kb.ps(f"bank{i}") for i in range(8)]
    PJ = [banks[0], banks[1]]
    MMB = [banks[3], banks[4], banks[5], banks[6]]
    NEB = MMB
    CHB = [banks[7], banks[2]]
    PQ = [banks[0], banks[1], banks[3], banks[4], banks[5], banks[6]]
    rr = {"pj": 0, "mm": 0, "ne": 0, "ch": 0, "pq": 0}

    def bank(kind):
        pool = {"pj": PJ, "mm": MMB, "ne": NEB, "ch": CHB, "pq": PQ}[kind]
        if kind == "ne":
            kind = "mm"
        b = pool[rr[kind] % len(pool)]
        rr[kind] += 1
        return b

    class Pool_:
        def __init__(self, n, shape, dt, name):
            self.t = [kb.sb(shape, dt, f"{name}{i}") for i in range(n)]
            self.i = 0
            self.nbase = n
            self.n_active = n

        def get(self):
            t = self.t[self.i % self.n_active]
            self.i += 1
            return t

    def dbg_dump(name, view):
        if name in dbg_o:
            kb.dma("pool", dbg_o[name].all(), view, is_out=True)

    def rmsnorm_to_xnT(nsub, gname, gidx, out_bf=None, keep_xs=None):
        for st in range(nsub):
            ss = small[:, st:st + 1]
            kb.act(big.all(), x_tok[:, st, :], AF.Square, accum=ss)
            rs = small[:, 8 + st:9 + st]
            kb.ts(rs, ss, 1.0 / D, ALU.mult, NORM_EPS, ALU.add)
            kb.tt(rs, rs, neghalf[:, 0:1], ALU.pow, eng="pool")
            kb.act(big.all(), x_tok[:, st, :], AF.Copy, scale=rs)
            for half in range(2):
                b = bank("pj")
                kb.pe([kb.tr(b[:, q * 128:(q + 1) * 128], big[:, (half * 4 + q) * 128:(half * 4 + q + 1) * 128], ident.all()) for q in range(4)],
                      [big, ident], [b])
                for q in range(4):
                    c = half * 4 + q
                    dstv = xnT[:, c, 1 + st * 128:1 + (st + 1) * 128]
                    kb.ts(dstv, b[:, q * 128:(q + 1) * 128], vec(gname, gidx * 8 + c), ALU.mult)
                    if out_bf is not None:
                        kb.cp(out_bf[:, c, st * 128:(st + 1) * 128], dstv, eng="act")
            if keep_xs is not None:
                keep_xs(st)

    def out_proj(nsub, pk, j):
        for q in range(4):
            wp = [wload((pk, j, q, hf)) for hf in range(2)]
            for st in range(nsub):
                b = bank("pq")
                fns = []
                for c in range(NHP):
                    w = wp[c // 8].all().re("p (c n) -> p c n", n=256)
                    fns.append(kb.mm(b[:, 0:256], yzT[:, c, st * 128:(st + 1) * 128], w[:, c % 8, :], start=(c == 0), stop=(c == NHP - 1)))
                kb.pe(fns, [yzT, wp[0], wp[1]], [b])
                xv = x_tok[:, st, q * 256:(q + 1) * 256]
                kb.tt(xv, b[:, 0:256], xv, ALU.add)

    f32p = Pool_(22, [128, TT], F32, "f")
    bon_p = Pool_(2, [128, TT], F32, "bon")
    ARp = Pool_(2, [128, 2, TT], BF16, "ar")
    bkp = Pool_(4, [128, TT], BF16, "bk")
    tokp = Pool_(2, [128, NSUB, 3, 128], BF16, "tok")
    szp = Pool_(2, [128, TT], BF16, "sz")
    t3_p = Pool_(6, [128, 3, 128], BF16, "t3")
    m4_p = Pool_(6, [128, 4, 128], BF16, "m4")
    mt0_p = Pool_(4, [128, 128], BF16, "mt0")
    tfin_p = Pool_(6, [128, 128], BF16, "tfin")
    xu_p = Pool_(6, [128, 128], BF16, "xu")
    ytok_p = Pool_(4, [128, NSUB, 128], F32, "ytok")
    wc_p = Pool_(2, [128, NSEQ], F32, "wc")
    ubuf = kb.sb([128, 2 * NSUB * 512], F32, "ubuf")
    ures = [Res(), Res()]
    u_tiles = [kb.view(ubuf.h[:, i * NSUB * 512:(i + 1) * NSUB * 512].rearrange("p (s n) -> p s n", n=512), ures[i]) for i in range(2)]
    arx = kb.view(ubuf.h[:].bitcast(BF16).rearrange("p (q b t) -> p q b t", q=2, b=NSEQ), tuple(ures))
    pbuf = kb.sb([128, 4 * 4 * TT], BF16, "pbuf")
    pres = [Res() for _ in range(4)]
    pviews = [kb.view(pbuf.h[:, i * 4 * TT:(i + 1) * 4 * TT].rearrange("p (c t) -> p c t", t=TT), pres[i]) for i in range(4)]
    utx = kb.view(pbuf.h[:, 0:NSEQ * 128].rearrange("p (b f) -> p b f", f=128), (pres[0], pres[1]))
    vx = kb.view(pbuf.h[:, NSEQ * 128:2 * NSEQ * 128].rearrange("p (b f) -> p b f", f=128), (pres[2], pres[3]))
    pb_ = pbuf.h
    allp = tuple(pres)
    o_ = 0
    for i_ in range(2):
        ARp.t.append(kb.view(pb_[:, o_:o_ + 2 * TT].rearrange("p (q t) -> p q t", q=2), allp))
        o_ += 2 * TT
    for i_ in range(4):
        bkp.t.append(kb.view(pb_[:, o_:o_ + TT], allp))
        o_ += TT
    for i_ in range(2):
        tokp.t.append(kb.view(pb_[:, o_:o_ + NSUB * 384].rearrange("p (s q f) -> p s q f", q=3, f=128), allp))
        o_ += NSUB * 384
    for i_ in range(2):
        szp.t.append(kb.view(pb_[:, o_:o_ + TT], allp))
        o_ += TT
    assert o_ <= 4 * 4 * TT, o_
    ub_ = ubuf.h
    allu = tuple(ures)
    for i_ in range(2):
        bon_p.t.append(kb.view(ub_[:, i_ * TT:(i_ + 1) * TT], allu))
        wc_p.t.append(kb.view(ub_[:, 2 * TT + i_ * NSEQ:2 * TT + (i_ + 1) * NSEQ], allu))
    sbuf1 = kb.sb([128, 1024], F32, "sbuf1")
    spst = kb.view(sbuf1.h[:].rearrange("p (h n) -> p h n", n=512), sbuf1.res)
    STs = kb.view(sbuf1.h[:].rearrange("p (b i) -> p b i", i=HD), sbuf1.res)
    sbuf2 = kb.sb([128, NCH * TT], BF16, "sbuf2")
    xnb = kb.view(sbuf2.h[:].rearrange("p (c t) -> p c t", t=TT), sbuf2.res)
    STsb = kb.view(sbuf2.h[:, 0:NSEQ * HD].rearrange("p (b i) -> p b i", i=HD), sbuf2.res)
    s0stage = kb.view(stage.h[0:HD, :].rearrange("p (b f) -> p b f", f=128), stage.res)

    def rwkv_layer(j, sample, mt, last):
        tt = 128 if sample else TT
        nsub = 1 if sample else NSUB
        nchunk = nsub
        nlev = 3 if sample else 7
        sfx = "_s" if sample else "_p"
        MS, MI, MST = msk["ms" + sfx], msk["mi" + sfx], msk["mst" + sfx]
        MSI4 = msi4[sfx]
        NFm = nf["nf" + sfx]

        def keep(st):
            if sample:
                gb = stage[:, D:2 * D]
                kb.dma("sp", gb, V(rnorm_d.h[j].partition_broadcast(128), rnorm_d.res))
                kb.tt(gb, big.all(), gb, ALU.mult)
                kb.dma("pool", shs_o[j, :, :], V(stage.h[7:128:8, D:2 * D], stage.res), is_out=True)
            elif last and st == nsub - 1:
                gb = stage[:, D:2 * D]
                kb.dma("sp", gb, V(rnorm_d.h[j].partition_broadcast(128), rnorm_d.res))
                kb.tt(gb, big.all(), gb, ALU.mult)
                kb.dma("pool", shp_o[j:j + 1, :], stage[127:128, D:2 * D], is_out=True)

        if not sample:
            if mt == 0:
                kb.op("dve", lambda e: e.memset(shiftc[j].all().ap, 0.0), [], [shiftc[j]])
            kb.cp(xnT[:, :, 0:1], shiftc[j].all())
        rmsnorm_to_xnT(nsub, "rnorm", j, keep_xs=keep)
        if not sample:
            kb.tt(dx[:, :, 0:tt], xnT[:, :, 0:tt], xnT[:, :, 1:tt + 1], ALU.subtract)
            kb.cp(shiftc[j].all(), xnT[:, :, tt:tt + 1])
        else:
            sst = stage[0:NSEQ, 0:D]
            kb.dma("sp", sst, sshift[j, :, :])
            b = bank("pj")
            kb.pe([kb.tr(b[:, c * NSEQ:(c + 1) * NSEQ], stage[0:NSEQ, c * 128:(c + 1) * 128], ident[0:NSEQ, 0:NSEQ]) for c in range(NCH)],
                  [stage, ident], [b])
            shT = f32p.get()
            kb.cp(shT[:, 0:NCH * NSEQ], b[:, 0:NCH * NSEQ])
            xn4 = xnT[:, :, 1:129].re("p c (b t) -> p c b t", t=TS)
            dx4 = dx[:, :, 0:128].re("p c (b t) -> p c b t", t=TS)
            sh3 = shT[:, 0:NCH * NSEQ].re("p (c b) -> p c b", b=NSEQ)
            for c in range(NCH):
                kb.tt(dx4[:, c, :, 1:TS], xn4[:, c, :, 0:TS - 1], xn4[:, c, :, 1:TS], ALU.subtract)
                kb.tt(dx4[:, c, :, 0], sh3[:, c, :], xn4[:, c, :, 0], ALU.subtract)
        if cut == 1:
            raise StopEmit()
        def mix(mi):
            for c in range(NCH):
                kb.stt(xm[mi][:, c, 0:tt], dx[:, c, 0:tt], vec("mu", (j * 6 + mi) * 8 + c), xnT[:, c, 1:tt + 1], ALU.mult, ALU.add)

        l1n = 192 if j == 0 else 256
        l1 = wload(("l1", j), parts=[lambda a: a.rearrange("p (c n) -> p c n", n=256)[:, :, 0:l1n]]).all().re("p (c n) -> p c n", n=256)

        def lora_mid(li, c0, ncol, mix_id, fn):
            b = bank("pj")
            kb.pe([kb.mm(b[0:ncol, 0:tt], l1[:, c, c0:c0 + ncol], xm[mix_id][:, c, 0:tt], start=(c == 0), stop=(c == NCH - 1)) for c in range(NCH)],
                  [l1, xm[mix_id]], [b])
            kb.act(mids[0:ncol, li, 0:tt], b[0:ncol, 0:tt], fn)

        mix(MW)
        lora_mid(0, 0, W_LORA, MW, AF.Tanh)
        mix(MA)
        lora_mid(1, 96, A_LORA, MA, AF.Copy)
        for mi in (MR, MK, MV, MG):
            mix(mi)
        if j == 1:
            lora_mid(2, 192, V_LORA, MV, AF.Copy)

        if cut == 2:
            raise StopEmit()
        states = {}

        def A_gen(g):
            wr, wk, wv, wz = (wload(("rw", j, nm, g)).all().re("p (c n) -> p c n", n=256) for nm in ("r", "k", "v", "z"))
            l2parts = [lambda a: a[0:W_LORA, 0:512]] + ([lambda a: a[0:V_LORA, 512:768]] if j == 1 else [])
            l2 = wload(("l2", j, g), parts=l2parts).all().re("p (c n) -> p c n", n=256)
            hp_state = [None, None]

            def prep_gen(hl):
                hp = 2 * g + hl
                cs0 = hl * 128

                def proj(w, mix):
                    b = bank("pj")
                    kb.pe([kb.mm(b[:, 0:tt], w[:, c, cs0:cs0 + 128], xm[mix][:, c, 0:tt], start=(c == 0), stop=(c == NCH - 1)) for c in range(NCH)],
                          [w, xm[mix]], [b])
                    return b

                def lproj(li, rank):
                    b = bank("pj")
                    kb.pe([kb.mm(b[:, 0:tt], l2[0:rank, li, cs0:cs0 + 128], mids[0:rank, li, 0:tt])], [l2, mids], [b])
                    return b

                S_t, a_t, r_t, kk_t, kh_t, v_t, q_t, tA, tB, b_t, cs_t = (f32p.get() for _ in range(11))

                def sigm(dst, src, hb):
                    kb.act(dst[:, 0:tt], src[:, 0:tt], AF.Tanh, bias=hb, scale=0.5)
                    kb.ts(dst[:, 0:tt], dst[:, 0:tt], 0.5, ALU.mult, 0.5, ALU.add, eng="pool")
                b = lproj(0, W_LORA)
                sigm(S_t, b, hbias[:, j * 16 + hp:j * 16 + hp + 1])
                b = lproj(1, A_LORA)
                sigm(a_t, b, hbias[:, 32 + j * 16 + hp:32 + j * 16 + hp + 1])
                yield "sub"
                b = proj(wr, MR)
                kb.cp(r_t[:, 0:tt], b[:, 0:tt], eng="act")
                yield "sub"
                b = proj(wk, MK)
                kb.ts(kk_t[:, 0:tt], b[:, 0:tt], vec("k_k", j * 16 + hp), ALU.mult)
                kb.ts(tA[:, 0:tt], a_t[:, 0:tt], vec("k_a", j * 16 + hp), ALU.mult, omka[:, j * 16 + hp:j * 16 + hp + 1], ALU.add)
                kb.tt(kh_t[:, 0:tt], b[:, 0:tt], tA[:, 0:tt], ALU.mult)
                yield "sub"
                b = proj(wv, MV)
                if j == 0:
                    kb.cp(v_t[:, 0:tt], b[:, 0:tt], eng="act")
                    kb.cp(vfirst[:, hp, 0:tt], v_t[:, 0:tt])
                else:
                    bg = lproj(2, V_LORA)
                    sigm(tA, bg, hbias[:, 64 + hp:64 + hp + 1])
                    kb.tt(tB[:, 0:tt], vfirst[:, hp, 0:tt], b[:, 0:tt], ALU.subtract)
                    kb.tt(tB[:, 0:tt], tB[:, 0:tt], tA[:, 0:tt], ALU.mult)
                    kb.tt(v_t[:, 0:tt], b[:, 0:tt], tB[:, 0:tt], ALU.add)
                yield "sub"
                sz_t = szp.get()
                b = proj(wz, MG)
                kb.act(tA[:, 0:tt], b[:, 0:tt], AF.Tanh, scale=0.5)
                kb.stt(sz_t[:, 0:tt], tA[:, 0:tt], 1.0, b[:, 0:tt], ALU.add, ALU.mult)
                yield "sub"
                bon_t = bon_p.get()
                kb.stt(q_t[:, 0:tt], r_t[:, 0:tt], vec("r_k", j * 16 + hp), kh_t[:, 0:tt], ALU.mult, ALU.mult)
                b = bank("pj")
                kb.pe([kb.mm(b[:, 0:tt], blk2.all(), q_t[:, 0:tt])], [blk2, q_t], [b])
                kb.tt(bon_t[:, 0:tt], b[:, 0:tt], v_t[:, 0:tt], ALU.mult)
                yield "sub"
                kb.act(q_t[:, 0:tt], kk_t[:, 0:tt], AF.Square)
                kb.op("dve", lambda e: e.tensor_tensor_scan(out=cs_t[:, 0:tt].ap, data0=NFm[:, 0:tt].ap, data1=S_t[:, 0:tt].ap,
                                                            initial=0.0, op0=ALU.mult, op1=ALU.add), [NFm, S_t], [cs_t])
                yield "stage"
                bss = bank("pj")
                kb.pe([kb.mm(bss[:, 0:tt], blk2.all(), q_t[:, 0:tt])], [blk2, q_t], [bss])
                kb.act(q_t[:, 0:tt], bss[:, 0:tt], AF.Sqrt)
                yield "stage"
                n_t = q_t
                kb.ts(n_t[:, 0:tt], n_t[:, 0:tt], 1e-12, ALU.max)
                kb.op("dve", lambda e: e.reciprocal(out=n_t[:, 0:tt].ap, in_=n_t[:, 0:tt].ap), [n_t], [n_t])
                kb.tt(kk_t[:, 0:tt], kk_t[:, 0:tt], n_t[:, 0:tt], ALU.mult)
                kb.tt(b_t[:, 0:tt], kk_t[:, 0:tt], a_t[:, 0:tt], ALU.mult, eng="pool")
                e_t, e2_t = tA, tB
                ar = ARp.get()
                kb.act(e_t[:, 0:tt], cs_t[:, 0:tt], AF.Exp, scale=-C0)
                kb.tt(ar[:, 1, 0:tt], r_t[:, 0:tt], e_t[:, 0:tt], ALU.mult, eng="pool")
                kb.tt(e2_t[:, 0:tt], cs_t[:, 0:tt], S_t[:, 0:tt], ALU.subtract)
                kb.act(e2_t[:, 0:tt], e2_t[:, 0:tt], AF.Exp, scale=-C0)
                kb.stt(ar[:, 0, 0:tt], kk_t[:, 0:tt], -1.0, e2_t[:, 0:tt], ALU.mult, ALU.mult)
                yield "sub"
                kT, bT = bkp.get(), bkp.get()
                kb.act(e_t[:, 0:tt], cs_t[:, 0:tt], AF.Exp, scale=C0)
                kb.tt(kT[:, 0:tt], kh_t[:, 0:tt], e_t[:, 0:tt], ALU.mult, eng="pool")
                kb.tt(bT[:, 0:tt], b_t[:, 0:tt], e_t[:, 0:tt], ALU.mult, eng="pool")
                yield "sub"
                wc = wc_p.get()
                if sample:
                    cs3 = cs_t[:, 0:128].re("p (b t) -> p b t", t=TS)
                    kb.tt(e2_t[:, 0:128].re("p (b t) -> p b t", t=TS), cs3, cs3[:, :, TS - 1:TS].bc([128, NSEQ, TS]), ALU.subtract)
                    kb.act(wc[:, 0:NSEQ], cs3[:, :, TS - 1], AF.Exp, scale=-C0)
                else:
                    for c in range(nchunk):
                        kb.ts(e2_t[:, c * 128:(c + 1) * 128], cs_t[:, c * 128:(c + 1) * 128], cs_t[:, c * 128 + 127:c * 128 + 128], ALU.subtract)
                        kb.act(wc[:, c:c + 1], cs_t[:, c * 128 + 127:c * 128 + 128], AF.Exp, scale=-C0)
                kb.act(e2_t[:, 0:tt], e2_t[:, 0:tt], AF.Exp, scale=C0)
                khat, bhat = S_t, e_t
                kb.tt(khat[:, 0:tt], kh_t[:, 0:tt], e2_t[:, 0:tt], ALU.mult, eng="pool")
                kb.tt(bhat[:, 0:tt], b_t[:, 0:tt], e2_t[:, 0:tt], ALU.mult, eng="pool")
                yield "sub"
                tok3 = tokp.get()
                for c in range(nchunk):
                    b = bank("pj")
                    srcs = (v_t, khat, bhat)
                    kb.pe([kb.tr(b[:, i * 128:(i + 1) * 128], srcs[i][:, c * 128:(c + 1) * 128], ident.all()) for i in range(3)],
                          [v_t, khat, bhat, ident], [b])
                    kb.cp(tok3[:, c, :, :].re("p q f -> p (q f)"), b[:, 0:384], eng="act")
                Vt = kb.view(tok3.h[:, :, 0, :], tok3.res)
                Kt = kb.view(tok3.h[:, :, 1, :], tok3.res)
                Bt = kb.view(tok3.h[:, :, 2, :], tok3.res)
                hp_state[hl] = dict(hp=hp, ar=ar, kT=kT, bT=bT, Vt=Vt, Kt=Kt, Bt=Bt, wc=wc, bon=bon_t, sz=sz_t)

            pg = [prep_gen(0), prep_gen(1)]
            for _stage in range(3):
                for gq in pg:
                    while True:
                        try:
                            tag = next(gq)
                        except StopIteration:
                            break
                        yield
                        if tag == "stage":
                            break
            states[g] = hp_state

        def B_gen(g):
            hp_state = states[g]
            def scan_gen(hs, hl):
                hp = hs["hp"]
                ar, kT, bT, Vt, Kt, Bt, wc = hs["ar"], hs["kT"], hs["bT"], hs["Vt"], hs["Kt"], hs["Bt"], hs["wc"]
                ytok = ytok_p.get()
                if sample:
                    for h in range(2):
                        src = swkv.h[j, :, 2 * hp + h, :, :].rearrange("b i j -> i b j")
                        kb.dma("sp", s0stage[:, :, h * HD:(h + 1) * HD], V(src, swkv.res))
                    for q4 in range(4):
                        b = bank("pj")
                        kb.pe([kb.tr(b[:, q * HD:(q + 1) * HD], s0stage[:, q4 * 4 + q, :], ident[0:HD, 0:HD]) for q in range(4)], [s0stage, ident], [b])
                        kb.cp(STs[:, q4 * 4:(q4 + 1) * 4, :].re("p b i -> p (b i)"), b[:, 0:4 * HD])
                    kb.cp(STsb.all(), STs.all(), eng="act")
                    for q in range(2):
                        kb.tt(arx[:, q, :, :], ar[:, q, 0:128].re("p (o t) -> p o t", o=1).bc([128, NSEQ, 128]), selb.all(), ALU.mult)
                elif mt == 0:
                    kb.op("dve", lambda e: e.memset(ST[j][:, hp, :].ap, 0.0), [], [ST[j]])
                    kb.op("dve", lambda e: e.memset(STb[j][:, hp, :].ap, 0.0), [], [STb[j]])
                tres = {}

                def tphase(c):
                    cols = slice(c * 128, (c + 1) * 128)
                    Tm, Mak, Mbr, Mkr = [], [], [], []
                    bmt = bank("pj")
                    cur = []
                    for h in range(2):
                        P = slice(64 * h, 64 * h + 64)
                        bm = bank("mm")
                        kb.pe([kb.mm(bm[:, 0:256].re("p (q t) -> p q t", q=2), bT[P, cols], ar[P, :, cols]),
                               kb.mm(bm[:, 256:512].re("p (q t) -> p q t", q=2), kT[P, cols], ar[P, :, cols])],
                              [bT, kT, ar], [bm])
                        kb.pe([kb.mm(bmt[:, h * 128:(h + 1) * 128], ar[P, 0, cols], bT[P, cols])], [ar, bT], [bmt])
                        m4 = m4_p.get()
                        mt0 = mt0_p.get()
                        kb.tt(m4.all(), bm[:, 0:512].re("p (q t) -> p q t", q=4), MSI4.all(), ALU.mult)
                        kb.tt(mt0.all(), bmt[:, h * 128:(h + 1) * 128], MST.all(), ALU.mult)
                        Mbr.append(m4[:, 1, :])
                        Mak.append(m4[:, 2, :])
                        Mkr.append(m4[:, 3, :])
                        cur.append((m4, mt0))
                        yield
                    for lev in range(nlev):
                        lastl = lev == nlev - 1
                        for h in range(2):
                            bn = bank("ne")
                            if lev == 0:
                                m4, mt0 = cur[h]
                                M0 = m4[:, 0, :]
                                kb.pe([kb.mm(bn[:, 0:128], mt0.all(), M0, start=True, stop=True),
                                       kb.mm(bn[:, 128:256], mt0.all(), identb.all(), start=True, stop=False),
                                       kb.mm(bn[:, 128:256], identb.all(), identb.all(), start=False, stop=True),
                                       kb.mm(bn[:, 256:384], M0, mt0.all(), start=True, stop=True)], [m4, mt0, identb], [bn])
                                t3n = t3_p.get()
                                kb.cp(t3n.all().re("p q t -> p (q t)"), bn[:, 0:384], eng=("act" if h % 2 == 0 else "dve"))
                                cur[h] = t3n
                                continue
                            t3 = cur[h]
                            if not lastl:
                                kb.pe([kb.mm(bn[:, 0:256].re("p (q t) -> p q t", q=2), t3[:, 2, :], t3[:, 0:2, :], start=True, stop=False),
                                       kb.mm(bn[:, 128:256], identb.all(), t3[:, 1, :], start=False, stop=True),
                                       kb.mm(bn[:, 256:384], t3[:, 0, :], t3[:, 2, :], start=True, stop=True)], [t3, identb], [bn])
                                t3n = t3_p.get()
                                kb.cp(t3n.all().re("p q t -> p (q t)"), bn[:, 0:384], eng=("dve" if (2 * lev + h) % 3 == 0 else "act"))
                                cur[h] = t3n
                            else:
                                kb.pe([kb.mm(bn[:, 0:128], t3[:, 2, :], t3[:, 1, :], start=True, stop=False),
                                       kb.mm(bn[:, 0:128], identb.all(), t3[:, 1, :], start=False, stop=True)], [t3, identb], [bn])
                                t_T = tfin_p.get()
                                kb.cp(t_T.all(), bn[:, 0:128], eng=("act" if h == 0 else "dve"))
                                Tm.append(t_T.all())
                        yield

                    tres[c] = (cols, Tm, Mak, Mbr, Mkr)

                def chain(c):
                    cols, Tm, Mak, Mbr, Mkr = tres[c]
                    stb = STsb if sample else STb[j]
                    bx = bank("ch")
                    fns = []
                    for h in range(2):
                        P = slice(64 * h, 64 * h + 64)
                        oc = slice(64 * h, 64 * h + 64)
                        if sample:
                            for bq in range(NSEQ):
                                fns.append(kb.mm(bx[:, oc], arx[P, 0, bq, :], stb[P, bq, :], start=(bq == 0), stop=False))
                        else:
                            fns.append(kb.mm(bx[:, oc], ar[P, 0, cols], stb[P, hp, :], start=True, stop=False))
                        fns.append(kb.mm(bx[:, oc], Mak[h], Vt[:, c, oc], start=False, stop=True))
                    kb.pe(fns, [ar, arx, stb, Vt, Mak[0], Mak[1]], [bx])
                    xtb = xu_p.get()
                    kb.cp(xtb.all(), bx[:, 0:128], eng="act")
                    yield
                    bu = bank("ch")
                    kb.pe([kb.mm(bu[:, 64 * h:64 * h + 64], Tm[h], xtb[:, 64 * h:64 * h + 64]) for h in range(2)], [Tm[0], Tm[1], xtb], [bu])
                    utb = xu_p.get()
                    kb.cp(utb.all(), bu[:, 0:128], eng="act")
                    yield
                    by = bank("ch")
                    fns = []
                    for h in range(2):
                        P = slice(64 * h, 64 * h + 64)
                        oc = slice(64 * h, 64 * h + 64)
                        if sample:
                            for bq in range(NSEQ):
                                fns.append(kb.mm(by[:, oc], arx[P, 1, bq, :], stb[P, bq, :], start=(bq == 0), stop=False))
                        else:
                            fns.append(kb.mm(by[:, oc], ar[P, 1, cols], stb[P, hp, :], start=True, stop=False))
                        fns.append(kb.mm(by[:, oc], Mbr[h], utb[:, oc], start=False, stop=False))
                        fns.append(kb.mm(by[:, oc], Mkr[h], Vt[:, c, oc], start=False, stop=True))
                    kb.pe(fns, [ar, arx, stb, utb, Vt, Mbr[0], Mbr[1], Mkr[0], Mkr[1]], [by])
                    kb.cp(ytok[:, c, :], by[:, 0:128], eng="act")
                    yield
                    if hp == 0 and mt == 0 and j == 0 and not sample and c == 0:
                        dbg_dump("T0", Tm[0]); dbg_dump("Mak0", Mak[0]); dbg_dump("Mbr0", Mbr[0]); dbg_dump("Mkr0", Mkr[0])
                        dbg_dump("XT", xtb.all()); dbg_dump("UT", utb.all()); dbg_dump("Vt", Vt[:, 0, :])
                        dbg_dump("arT", ar.all().re("p q t -> p (q t)")); dbg_dump("kT", kT.all()); dbg_dump("bT", bT.all())
                    if cut == 5 or cut == 60 + c:
                        raise StopEmit()
                    if sample:
                        kb.tt(utx.all(), utb.all().re("p (o f) -> p o f", o=1).bc([128, NSEQ, 128]), seltb.all().re("p (b o) -> p b o", o=1).bc([128, NSEQ, 128]), ALU.mult)
                        kb.tt(vx.all(), Vt[:, 0, :].re("p (o f) -> p o f", o=1).bc([128, NSEQ, 128]), seltb.all().re("p (b o) -> p b o", o=1).bc([128, NSEQ, 128]), ALU.mult)
                        for half in range(2):
                            bs = bank("pj")
                            fns = []
                            for h in range(2):
                                P = slice(64 * h, 64 * h + 64)
                                oc = slice(64 * h, 64 * h + 64)
                                o3 = bs[P, :].re("p (b i) -> p b i", i=HD)
                                fns.append(kb.mm(o3, Bt[:, 0, oc], utx[:, half * 8:(half + 1) * 8, oc], start=True, stop=False))
                                fns.append(kb.mm(o3, Kt[:, 0, oc], vx[:, half * 8:(half + 1) * 8, oc], start=False, stop=True))
                            kb.pe(fns, [Bt, Kt, utx, vx], [bs])
                            sv = STs[:, half * 8:(half + 1) * 8, :]
                            kb.tt(sv, sv, wc[:, half * 8:(half + 1) * 8].re("p (b o) -> p b o", o=1).bc([128, 8, HD]), ALU.mult)
                            kb.tt(sv, sv, bs[:, :].re("p (b i) -> p b i", i=HD), ALU.add)
                    else:
                        bs = bank("ch")
                        fns = []
                        for h in range(2):
                            P = slice(64 * h, 64 * h + 64)
                            oc = slice(64 * h, 64 * h + 64)
                            fns.append(kb.mm(bs[P, 0:HD], Bt[:, c, oc], utb[:, oc], start=True, stop=False))
                            fns.append(kb.mm(bs[P, 0:HD], Kt[:, c, oc], Vt[:, c, oc], start=False, stop=True))
                        kb.pe(fns, [Bt, Kt, utb, Vt], [bs])
                        if cut == 51:
                            raise StopEmit()
                        kb.stt(ST[j][:, hp, :], ST[j][:, hp, :], wc[:, c:c + 1], bs[:, 0:HD], ALU.mult, ALU.add)
                        if cut == 52:
                            raise StopEmit()
                        kb.cp(STb[j][:, hp, :], ST[j][:, hp, :], eng="act")
                    yield

                if nchunk > 1 and not sample and TPHASE_INTERLEAVE and cut is None:
                    tg = [tphase(c) for c in range(nchunk)]
                    while tg:
                        for t_ in list(tg):
                            try:
                                next(t_)
                                yield
                            except StopIteration:
                                tg.remove(t_)
                    for c in range(nchunk):
                        yield from chain(c)
                else:
                    for c in range(nchunk):
                        yield from tphase(c)
                        if cut == 4 or cut == 40 + c or (cut == 141 and c == 1):
                            raise StopEmit()
                        yield from chain(c)

                if cut == 6:
                    raise StopEmit()
                if sample or last:
                    nb = NSEQ if sample else 1
                    for b0 in range(0, nb, 2):
                        nq = min(2, nb - b0)
                        b = bank("pj")
                        srcv = (lambda q: STs[:, b0 + q, :]) if sample else (lambda q: ST[j][:, hp, :])
                        kb.pe([kb.tr(b[0:HD, q * 128:(q + 1) * 128], srcv(q), ident.all()) for q in range(nq)], [STs if sample else ST[j], ident], [b])
                        if sample:
                            so = f32p.get()
                        else:
                            yh_pre = ytok_p.get()
                            so = kb.view(yh_pre.h[:].rearrange("p s f -> p (s f)"), yh_pre.res)
                        kb.cp(so[0:HD, 0:nq * 128], b[0:HD, 0:nq * 128], eng="act")
                        if sample:
                            for h in range(2):
                                dst = wkvs_o.h[j, b0:b0 + nq, 2 * hp + h, :, :].rearrange("b i j -> i b j")
                                srcv = so.h[0:HD, 0:nq * 128].rearrange("i (b h j) -> i b h j", h=2, j=HD)[:, :, h, :]
                                kb.dma("pool", V(dst, wkvs_o.res), V(srcv, so.res), is_out=True)
                        else:
                            dst = wkvp_o.h[j, 2 * hp:2 * hp + 2, :, :].rearrange("h i j -> i h j")
                            kb.dma("pool", V(dst, wkvp_o.res), V(so.h[0:HD, 0:128].rearrange("i (h j) -> i h j", j=HD), so.res), is_out=True)

                if hp == 0 and mt == 0 and j == 0 and not sample:
                    dbg_dump("ytok", ytok.all().re("p s f -> p (s f)"))
                if cut == 7:
                    raise StopEmit()
                nst = nsub * 2
                y3 = ytok[:, 0:nsub, :].re("p s (h i) -> p (s h) i", i=HD)
                so_ = 4 * hl
                s1 = small[:, 16 + so_:16 + so_ + nst]
                s2 = small[:, 24 + so_:24 + so_ + nst]
                kb.op("dve", lambda e, y3=y3, s1=s1: e.tensor_reduce(out=s1.ap, in_=y3.ap, axis=AX.X, op=ALU.add), [ytok], [small])
                yield
                yh = yh_pre if (last and not sample) else ytok_p.get()
                sqv = yh.all().re("p s f -> p (s f)")
                kb.act(sqv[:, 0:nsub * 128], ytok[:, 0:nsub, :].re("p s f -> p (s f)"), AF.Square)
                kb.op("dve", lambda e, sqv=sqv, s2=s2: e.tensor_reduce(out=s2.ap, in_=sqv[:, 0:nsub * 128].re("p (a i) -> p a i", i=HD).ap, axis=AX.X, op=ALU.add), [yh], [small])
                mean = small[:, 32 + so_:32 + so_ + nst]
                var = small[:, 40 + so_:40 + so_ + nst]
                yield
                kb.ts(mean, s1, 1.0 / HD, ALU.mult)
                kb.tt(var, mean, mean, ALU.mult)
                kb.stt(var, s2, 1.0 / HD, var, ALU.mult, ALU.subtract)
                kb.ts(var, var, GN_EPS, ALU.add)
                kb.tt(var, var, neghalf[:, 0:nst], ALU.pow, eng="pool")
                yield
                yh3 = yh[:, 0:nsub, :].re("p s (h i) -> p (s h) i", i=HD)
                kb.tt(yh3, y3, mean.re("p (a o) -> p a o", o=1).bc([128, nst, HD]), ALU.subtract)
                kb.tt(yh3, yh3, var.re("p (a o) -> p a o", o=1).bc([128, nst, HD]), ALU.mult)
                yield
                b = bank("pj")
                kb.pe([kb.tr(b[:, s * 128:(s + 1) * 128], yh[:, s, :], ident.all()) for s in range(nsub)], [yh, ident], [b])
                ya = kb.view(ytok.h[:].rearrange("p s f -> p (s f)"), ytok.res)
                kb.act(ya[:, 0:tt], b[:, 0:tt], AF.Identity, scale=vec("gn_g", j * 16 + hp), bias=vec("gn_b", j * 16 + hp))
                kb.tt(ya[:, 0:tt], ya[:, 0:tt], hs["bon"][:, 0:tt], ALU.add)
                kb.stt(yzT[:, hp, 0:tt], ya[:, 0:tt], 0.5, hs["sz"][:, 0:tt], ALU.mult, ALU.mult)

            gens = [scan_gen(hs, hl) for hl, hs in enumerate(hp_state)]
            if sample:
                for gnr in gens:
                    for _ in gnr:
                        yield
                gens = []
            while gens:
                for gnr in list(gens):
                    try:
                        next(gnr)
                        yield
                    except StopIteration:
                        gens.remove(gnr)

        def run_all(gen):
            for _ in gen:
                pass

        pools_ab = (ARp, bkp, tokp, szp, bon_p, wc_p)
        overlap = (not sample) and cut is None and OVERLAP_AB
        for p_ in pools_ab:
            p_.n_active = len(p_.t) if overlap else p_.nbase
        if not overlap:
            for g in range(NG):
                run_all(A_gen(g))
                if cut == 3:
                    raise StopEmit()
                run_all(B_gen(g))
        else:
            run_all(A_gen(0))
            for g in range(NG):
                bgen = B_gen(g)
                agen = A_gen(g + 1) if g + 1 < NG else None
                done_a, done_b = agen is None, False
                while not (done_a and done_b):
                    for _ in range(AB_RATIO):
                        if not done_b:
                            try:
                                next(bgen)
                            except StopIteration:
                                done_b = True
                    if not done_a:
                        try:
                            next(agen)
                        except StopIteration:
                            done_a = True

        if mt == 0 and j == 0 and not sample:
            dbg_dump("yz", yzT.all().re("p c t -> p (c t)"))
            dbg_dump("xn", xnT.all().re("p c t -> p (c t)"))
        out_proj(nsub, "rwo", j)

    class RR:
        def __init__(self, ts):
            self.t = ts
            self.i = 0

        def get(self):
            t = self.t[self.i % len(self.t)]
            self.i += 1
            return t
    u_p = RR(u_tiles)
    szg_p = RR([pviews[0], pviews[1]])
    pT_p = RR([pviews[2], pviews[3]])

    def pool_layer(j, sample, mt, last):
        tt = 128 if sample else TT
        nsub = 1 if sample else NSUB
        rmsnorm_to_xnT(nsub, "pnorm", j, out_bf=xnb)
        for g in range(4):
            wu = [wload(("pu", j, g, hf)).all().re("p (c n) -> p c n", n=256) for hf in range(2)]
            wzz = [wload(("pz", j, g, hf)).all().re("p (c n) -> p c n", n=256) for hf in range(2)]
            wg = wload(("pg", j, g)).all().re("p (c n) -> p c n", n=512)
            u = u_p.get()
            for st in range(nsub):
                for hf in range(2):
                    b = bank("pq")
                    kb.pe([kb.mm(b[:, 0:256], xnb[:, c, st * 128:(st + 1) * 128], wu[hf][:, c, :], start=(c == 0), stop=(c == NCH - 1)) for c in range(NCH)],
                          [xnb, wu[hf]], [b])
                    kb.cp(u[:, st, hf * 256:(hf + 1) * 256], b[:, 0:256], eng="act")
            szg = szg_p.get()
            for cc in range(4):
                b = bank("pq")
                w = wzz[cc // 2]
                kb.pe([kb.mm(b[:, 0:tt], w[:, c, (cc % 2) * 128:(cc % 2) * 128 + 128], xnb[:, c, 0:tt], start=(c == 0), stop=(c == NCH - 1)) for c in range(NCH)],
                      [xnb, w], [b])
                th_ = f32p.get()
                kb.act(th_[:, 0:tt], b[:, 0:tt], AF.Tanh, scale=0.5)
                kb.stt(szg[:, cc, 0:tt], th_[:, 0:tt], 1.0, b[:, 0:tt], ALU.add, ALU.mult)
            if sample:
                for hf in range(2):
                    src = spool.h[j, hf * 8:(hf + 1) * 8, :, g * 512:(g + 1) * 512].rearrange("b r n -> (b r) n")
                    kb.dma("sp", spst[0:8 * PBUF, hf, :], V(src, spool.res))
            pT = pT_p.get()
            pb0 = 32 * (g % 2)
            for cc in range(4):
                b = bank("pq")
                fns = []
                rd = [u, cc_[j], spst]
                for st in range(nsub):
                    o = b[:, st * 128:(st + 1) * 128]
                    uc = u[:, st, cc * 128:(cc + 1) * 128]
                    if sample:
                        fns.append(kb.mm(o, uc, bands["bsc"][:, g, :], start=True, stop=False))
                        fns.append(kb.mm(o, spst[0:8 * PBUF, 0, cc * 128:(cc + 1) * 128], bsp[0:8 * PBUF, g * 2 + 0, :], start=False, stop=False))
                        fns.append(kb.mm(o, spst[0:8 * PBUF, 1, cc * 128:(cc + 1) * 128], bsp[0:8 * PBUF, g * 2 + 1, :], start=False, stop=True))
                    elif mt == 0 and st == 0:
                        fns.append(kb.mm(o, uc, bands["bf"][:, g, :], start=True, stop=True))
                    else:
                        fns.append(kb.mm(o, uc, bands["bc"][:, g, :], start=True, stop=False))
                        if st > 0:
                            prev = u[64:128, st - 1, cc * 128:(cc + 1) * 128]
                            fns.append(kb.mm(o, prev, bprev64[:, g, :], start=False, stop=True))
                        else:
                            prev = cc_[j][pb0:pb0 + 16, (g // 2) * 512 + cc * 128:(g // 2) * 512 + (cc + 1) * 128]
                            fns.append(kb.mm(o, prev, bpc[pb0:pb0 + 16, g, :], start=False, stop=True))
                kb.pe(fns, rd + [bands["bsc"], bands["bf"], bands["bc"], bpc, bprev64, bsp], [b])
                kb.cp(pT[:, cc, 0:tt], b[:, 0:tt], eng="act")
            if sample:
                kb.dma("pool", pools_u[j, :, g * 512:(g + 1) * 512], u[:, 0, :], is_out=True)
                if g == 0:
                    kb.dma("pool", pools_old[j], V(spool.h[j, :, 8:15, :], spool.res), is_out=True)
            else:
                if last:
                    kb.dma("pool", poolp_o[j, :, g * 512:(g + 1) * 512], u[128 - PBUF:128, nsub - 1, :], is_out=True)
                else:
                    kb.dma("sp", cc_[j][pb0:pb0 + 16, (g // 2) * 512:(g // 2 + 1) * 512], u[112:128, nsub - 1, :])
            for oc in range(4):
                b = bank("pq")
                kb.pe([kb.mm(b[:, 0:tt], wg[:, ic, oc * 128:(oc + 1) * 128], pT[:, ic, 0:tt], start=(ic == 0), stop=(ic == 3)) for ic in range(4)],
                      [wg, pT], [b])
                ya = f32p.get()
                kb.act(ya[:, 0:tt], b[:, 0:tt], AF.Identity, scale=vec("pscale", j * 16 + g * 4 + oc), bias=pbs[:, j * 16 + g * 4 + oc:j * 16 + g * 4 + oc + 1])
                kb.stt(yzT[:, g * 4 + oc, 0:tt], ya[:, 0:tt], 0.5, szg[:, oc, 0:tt], ALU.mult, ALU.mult)
        out_proj(nsub, "pwo", j)


    def final(sample, mt):
        nsub = 1 if sample else NSUB
        for st in range(nsub):
            ss = small[:, st:st + 1]
            kb.act(big.all(), x_tok[:, st, :], AF.Square, accum=ss)
            rs = small[:, 8 + st:9 + st]
            kb.ts(rs, ss, 1.0 / D, ALU.mult, NORM_EPS, ALU.add)
            kb.tt(rs, rs, neghalf[:, 0:1], ALU.pow, eng="pool")
            yo = stage[:, D:2 * D]
            kb.dma("sp", big.all(), V(fnorm_d.h.partition_broadcast(128), fnorm_d.res))
            kb.stt(yo, x_tok[:, st, :], rs, big.all(), ALU.mult, ALU.mult)
            if sample:
                kb.dma("pool", y_s.all(), yo, is_out=True)
            else:
                r0 = mt * TT + st * 128
                kb.dma("pool", y_p[r0:r0 + 128, :], yo, is_out=True)

    setup()
    prepass()
    tiles = [(False, m) for m in range(n_mt)] + ([(True, 0)] if do_sample else [])
    for (sample, mt) in tiles:
        last = (not sample) and (mt == NMT - 1)
        if sample:
            kb.dma("sp", x_tok[:, 0, :], xs.all())
        else:
            for st in range(NSUB):
                r0 = mt * TT + st * 128
                kb.dma("sp", x_tok[:, st, :], xp[r0:r0 + 128, :])
        for layer in range(4):
            if stop_after is not None and layer > stop_after:
                break
            try:
                if layer % 2 == 0:
                    rwkv_layer(layer // 2, sample, mt, last)
                else:
                    pool_layer(layer // 2, sample, mt, last)
            except StopEmit:
                break
        if "x_res" in dbg_o and not sample and mt == 0:
            kb.dma("sp", dbg_o["x_res"].all().re("(s p) d -> p s d", p=128), x_tok.all(), is_out=True)
        final(sample, mt)
    for tok in kb.out_toks:
        kb._wait("sp", tok)
    import os
    if os.environ.get("KDBG"):
        print("sem counts", kb.cnt, "dma", kb.dcnt, flush=True)
        print("sbuf remaining", nc.sbuf_bytes_remaining, flush=True)
    kb.es.close()
    return nc


_WNAMES = ["rwkv_w_r", "rwkv_w_k", "rwkv_w_v", "rwkv_w_z", "rwkv_w1", "rwkv_w2", "rwkv_a1", "rwkv_a2", "rwkv_v1",
           "rwkv_v2", "rwkv_w_o", "pool_w_in", "pool_w_grp", "pool_w_o"]


def make_in_maps(inp):
    consts = _consts()
    vecs = _pack_vecs(inp)
    shared = {"vecs": vecs, "final_norm": np.ascontiguousarray(inp["final_norm"], np.float32),
              "rwkv_norm": np.ascontiguousarray(inp["rwkv_norm"], np.float32)}
    for k, v in consts.items():
        shared["c_" + k] = np.ascontiguousarray(v, np.float32)
    for k in _WNAMES:
        shared[k] = np.ascontiguousarray(inp[k], np.float32)
    maps = []
    for c in range(NCORES):
        m = dict(shared)
        m["xp"] = np.ascontiguousarray(inp["x_prompt"][c], np.float32)
        m["xs"] = np.ascontiguousarray(inp["x_sample"][c * NSEQ:(c + 1) * NSEQ].reshape(128, D), np.float32)
        m["sshift"] = np.ascontiguousarray(inp["state_shift"][:, c * NSEQ:(c + 1) * NSEQ], np.float32)
        m["swkv"] = np.ascontiguousarray(inp["state_wkv"][:, c * NSEQ:(c + 1) * NSEQ], np.float32)
        m["spool"] = np.ascontiguousarray(inp["state_pool"][:, c * NSEQ:(c + 1) * NSEQ], np.float32)
        maps.append(m)
    return maps


def kernel(**inputs):
    inp = {k: np.asarray(v) for k, v in inputs.items()}
    nc = build()
    res = run_bass_kernel_spmd(nc, make_in_maps(inp), core_ids=list(range(NCORES))).results
    y_prompt = np.stack([r["y_p"] for r in res], 0).astype(np.float32)
    y_sample = np.concatenate([r["y_s"].reshape(NSEQ, TS, D) for r in res], 0).astype(np.float32)
    sh_p = np.stack([r["shp"] for r in res], 1).astype(np.float32)
    wkv_p = np.stack([r["wkvp"] for r in res], 1).astype(np.float32)
    pool_p = np.stack([r["poolp"] for r in res], 1).astype(np.float32)
    sh_s = np.concatenate([r["shs"] for r in res], 1).astype(np.float32)
    wkv_s = np.concatenate([r["wkvs"] for r in res], 1).astype(np.float32)
    pool_s = np.concatenate(
        [np.concatenate([r["pools_old"], r["pools_u"].reshape(2, NSEQ, TS, C)], axis=2) for r in res], 1).astype(np.float32)
    return (y_prompt, y_sample, sh_p, wkv_p, pool_p, sh_s, wkv_s, pool_s)
```

```python
import numpy as np
import concourse.bass as bass
import concourse.mybir as mybir
from concourse.bass_utils import run_bass_kernel_spmd
from contextlib import ExitStack

F32 = mybir.dt.float32
BF16 = mybir.dt.bfloat16
AF = mybir.ActivationFunctionType
ALU = mybir.AluOpType
AX = mybir.AxisListType

NCORES = 8
D = 1024
C = 2048
NH = 32
HD = 64
SEQ = 2048
TS = 8
NSEQ = 16
NCH = D // 128
NHP = C // 128
W_LORA = 96
A_LORA = 96
V_LORA = 64
PBUF = 15
NSUB = 2
TT = 128 * NSUB
NMT = SEQ // TT
C0 = float(np.exp(-0.5))
NORM_EPS = 1e-6
GN_EPS = 64e-5
PAST_LEN = 16384
WINS = (2, 4, 8, 16)
SAME_ENG_SYNC = True
OVERLAP_AB = True
TPHASE_INTERLEAVE = False
AB_RATIO = 2
NO_SELF_SYNC = ("pe",)
NG = C // 256


class Res:
    __slots__ = ("w", "r", "excl")

    def __init__(self, excl=False):
        self.w = None
        self.r = {}
        self.excl = excl


class V:
    __slots__ = ("ap", "res")

    def __init__(self, ap, res):
        self.ap = ap
        self.res = res

    def re(self, pat, **kw):
        return V(self.ap.rearrange(pat, **kw), self.res)

    def bc(self, shape):
        return V(self.ap.to_broadcast(list(shape)), self.res)

    def __getitem__(self, k):
        return V(self.ap[k], self.res)


class Tile:
    def __init__(self, handle, res=None, is_ap=False):
        self.h = handle
        self.res = res or Res()
        self.is_ap = is_ap

    def __getitem__(self, k):
        return V(self.h[k], self.res)

    def all(self):
        return V(self.h if self.is_ap else self.h[:], self.res)


class KB:
    def __init__(self, nc):
        self.nc = nc
        self.es = ExitStack()
        self.E = dict(pe=nc.tensor, act=nc.scalar, dve=nc.vector, pool=nc.gpsimd, sp=nc.sync)
        self.cnt = {}
        self.semi = {e: 0 for e in self.E}
        self.cursem = {}
        for e in self.E:
            self._newsem(e)
        self.seen = {e: {} for e in self.E}
        self.nd = 12
        self.dsems = {q: [self.es.enter_context(nc.semaphore(f"d{q}{i}")) for i in range(self.nd)] for q in ("sp", "pool")}
        self.dcnt = {q: [0] * self.nd for q in ("sp", "pool")}
        self.dnext = {q: 0 for q in ("sp", "pool")}
        self.ntile = 0
        self.out_toks = []

    def _newsem(self, e):
        nm = f"s_{e}_{self.semi[e]}"
        s = self.es.enter_context(self.nc.semaphore(nm))
        self.semi[e] += 1
        self.cursem[e] = (s, nm)
        self.cnt[e] = 0

    def sb(self, shape, dt=F32, name=None):
        self.ntile += 1
        nm = name or f"t{self.ntile}"
        return Tile(self.es.enter_context(self.nc.sbuf_tensor(f"{nm}_{self.ntile}", list(shape), dt)))

    @staticmethod
    def view(ap, res):
        return Tile(ap, res=res, is_ap=True)

    def ps(self, name):
        return Tile(self.es.enter_context(self.nc.psum_tensor(name, [128, 512], F32)), res=Res(excl=True))

    def dram(self, name, shape, dt, kind):
        return Tile(self.nc.dram_tensor(name, list(shape), dt, kind=kind).ap(), is_ap=True)

    def _wait(self, e, tok):
        sem, key, val, te = tok
        if te == e and (e in NO_SELF_SYNC or not SAME_ENG_SYNC):
            return
        if self.seen[e].get(key, 0) >= val:
            return
        self.E[e].wait_ge(sem, val)
        self.seen[e][key] = val

    def _deps(self, e, reads, writes):
        for r in reads:
            if r.w is not None:
                self._wait(e, r.w)
        for w in writes:
            if w.w is not None:
                self._wait(e, w.w)
            for tok in w.r.values():
                self._wait(e, tok)

    def _mark(self, tok, reads, writes):
        for r in reads:
            r.r[tok[1]] = tok
        for w in writes:
            w.w = tok
            w.r = {}

    @staticmethod
    def _rl(xs):
        out = []
        for x in xs:
            if x is None or isinstance(x, (int, float)):
                continue
            r = x if isinstance(x, Res) else x.res
            if isinstance(r, (tuple, list)):
                out.extend(r)
            else:
                out.append(r)
        return out

    def op(self, e, fn, reads, writes):
        reads = self._rl(reads)
        writes = self._rl(writes)
        writes = writes + [r for r in reads if r.excl]
        reads = [r for r in reads if not r.excl]
        self._deps(e, reads, writes)
        inst = fn(self.E[e])
        if self.cnt[e] >= 30000:
            self._newsem(e)
        self.cnt[e] += 1
        sem, nm = self.cursem[e]
        inst.then_inc(sem, 1)
        tok = (sem, nm, self.cnt[e], e)
        self._mark(tok, reads, writes)
        return tok

    def dma(self, q, out, in_, extra_reads=(), is_out=False):
        reads = self._rl([in_] + list(extra_reads))
        writes = self._rl([out])
        k = self.dnext[q]
        self.dnext[q] = (k + 1) % self.nd
        sem = self.dsems[q][k]
        if self.dcnt[q][k] > 0:
            self._wait(q, (sem, f"d{q}{k}", self.dcnt[q][k] * 16, None))
        self._deps(q, reads, writes)
        inst = self.E[q].dma_start(out=out.ap, in_=in_.ap)
        self.dcnt[q][k] += 1
        inst.then_inc(sem, 16)
        tok = (sem, f"d{q}{k}", self.dcnt[q][k] * 16, None)
        self._mark(tok, reads, writes)
        if is_out:
            self.out_toks.append(tok)
        return tok

    @staticmethod
    def _a(x):
        return x.ap if isinstance(x, V) else x

    def act(self, out, in_, func, bias=None, scale=None, accum=None):
        kw = {}
        if bias is not None:
            kw["bias"] = self._a(bias)
        if scale is not None:
            kw["scale"] = self._a(scale)
        if accum is not None:
            kw["accum_out"] = accum.ap
        rd = [in_] + [x for x in (bias, scale) if isinstance(x, V)]
        wr = [out] + ([accum] if accum is not None else [])
        return self.op("act", lambda e: e.activation(out=out.ap, in_=in_.ap, func=func, **kw), rd, wr)

    def tt(self, out, in0, in1, op, eng="dve"):
        return self.op(eng, lambda e: e.tensor_tensor(out=out.ap, in0=in0.ap, in1=in1.ap, op=op), [in0, in1], [out])

    def ts(self, out, in0, s1, op0, s2=None, op1=None, eng="dve"):
        rd = [in0] + [x for x in (s1, s2) if isinstance(x, V)]
        if op1 is None:
            return self.op(eng, lambda e: e.tensor_scalar(out=out.ap, in0=in0.ap, scalar1=self._a(s1), scalar2=None, op0=op0), rd, [out])
        return self.op(eng, lambda e: e.tensor_scalar(out=out.ap, in0=in0.ap, scalar1=self._a(s1), scalar2=self._a(s2), op0=op0, op1=op1), rd, [out])

    def stt(self, out, in0, scalar, in1, op0, op1):
        rd = [in0, in1] + ([scalar] if isinstance(scalar, V) else [])
        return self.op("dve", lambda e: e.scalar_tensor_tensor(out=out.ap, in0=in0.ap, scalar=self._a(scalar), in1=in1.ap, op0=op0, op1=op1), rd, [out])

    def cp(self, out, in_, eng="dve"):
        if eng == "act":
            return self.act(out, in_, AF.Copy)
        return self.op(eng, lambda e: e.tensor_copy(out=out.ap, in_=in_.ap), [in_], [out])

    def pe(self, fns, reads, writes):
        def run(e):
            inst = None
            for f in fns:
                inst = f(e)
            return inst
        return self.op("pe", run, reads, writes)

    def mm(self, out, lhsT, rhs, start=True, stop=True):
        return lambda e: e.matmul(out.ap, lhsT=lhsT.ap, rhs=rhs.ap, start=start, stop=stop, skip_group_check=True)

    def tr(self, out, in_, ident):
        return lambda e: e.transpose(out.ap, in_.ap, ident.ap)


def _consts():
    c = {}
    s = np.arange(128)[:, None]
    t = np.arange(128)[None, :]
    same8 = (s // 8) == (t // 8)
    c["ident"] = np.eye(128, dtype=np.float32)
    c["blk2"] = ((s // 64) == (t // 64)).astype(np.float32)
    c["ms_p"] = (s < t).astype(np.float32)
    c["mi_p"] = (s <= t).astype(np.float32)
    c["mst_p"] = (t < s).astype(np.float32)
    c["ms_s"] = ((s < t) & same8).astype(np.float32)
    c["mi_s"] = ((s <= t) & same8).astype(np.float32)
    c["mst_s"] = ((t < s) & same8).astype(np.float32)
    nfp = np.ones((128, TT), np.float32)
    nfp[:, ::128] = 0.0
    nfs = np.ones((128, TT), np.float32)
    nfs[:, ::8] = 0.0
    c["nf_p"] = nfp
    c["nf_s"] = nfs
    sel = np.zeros((128, NSEQ, 128), np.float32)
    for b in range(NSEQ):
        sel[:, b, b * 8:(b + 1) * 8] = 1.0
    c["sel"] = sel
    selt = np.zeros((128, NSEQ), np.float32)
    for b in range(NSEQ):
        selt[b * 8:(b + 1) * 8, b] = 1.0
    c["selt"] = selt
    bc = np.zeros((4, 128, 128), np.float32)
    bf = np.zeros((4, 128, 128), np.float32)
    bp = np.zeros((4, 128, 128), np.float32)
    bsc = np.zeros((4, 128, 128), np.float32)
    bsp = np.zeros((4, 2, 128, 128), np.float32)
    for g, win in enumerate(WINS):
        inw = (s > t - win) & (s <= t)
        bc[g] = inw / win - np.eye(128)
        cnt = np.minimum(win, t + 1)
        bf[g] = inw / cnt - np.eye(128)
        bp[g] = ((s - 128) > (t - win)) / win
        bsc[g] = (inw & same8) / win - np.eye(128)
        for half in range(2):
            for b8 in range(8):
                b = half * 8 + b8
                for r in range(PBUF):
                    for tt in range(TS):
                        if r > PBUF + tt - win:
                            bsp[g, half, b8 * PBUF + r, b * 8 + tt] = 1.0 / win
    c["bc"] = np.ascontiguousarray(bc.transpose(1, 0, 2))
    c["bf"] = np.ascontiguousarray(bf.transpose(1, 0, 2))
    bpc = np.zeros((64, 4, 128), np.float32)
    for g in range(4):
        for r in range(16):
            bpc[32 * (g % 2) + r, g, :] = bp[g, 112 + r, :]
    c["bpc"] = bpc
    c["bprev64"] = np.ascontiguousarray(bp.transpose(1, 0, 2))
    c["bsc"] = np.ascontiguousarray(bsc.transpose(1, 0, 2))
    c["bsp"] = np.ascontiguousarray(bsp.transpose(2, 0, 1, 3)).reshape(128, 8, 128)
    return c


CONST_SHAPES = {k: v.shape for k, v in _consts().items()}

VEC_LAYOUT = [
    ("rnorm", 2 * 8), ("mu", 2 * 6 * 8), ("w0", 32), ("a0", 32), ("k_k", 32), ("k_a", 32), ("r_k", 32),
    ("gn_g", 32), ("gn_b", 32), ("v0", 16), ("pnorm", 16), ("pb", 32), ("pscale", 32),
]
VEC_OFF = {}
_o = 0
for _n, _w in VEC_LAYOUT:
    VEC_OFF[_n] = _o
    _o += _w
NVEC = _o


def _fm(a, n):
    a = np.asarray(a, np.float32)
    lead = a.shape[:-1]
    a = a.reshape(lead + (n, 128))
    a = np.moveaxis(a, -1, 0)
    return a.reshape(128, -1)


def _pack_vecs(inp):
    cols = [
        _fm(inp["rwkv_norm"], 8), _fm(inp["rwkv_mu"], 8), _fm(inp["rwkv_w0"], 16), _fm(inp["rwkv_a0"], 16),
        _fm(inp["rwkv_k_k"], 16), _fm(inp["rwkv_k_a"], 16), _fm(inp["rwkv_r_k"].reshape(2, C), 16),
        _fm(inp["rwkv_gn_g"], 16), _fm(inp["rwkv_gn_b"], 16), _fm(inp["rwkv_v0"], 16),
        _fm(inp["pool_norm"], 8), _fm(inp["pool_b_grp"].reshape(2, C), 16), _fm(inp["pool_scale"], 16),
    ]
    out = np.ascontiguousarray(np.concatenate(cols, axis=1))
    assert out.shape == (128, NVEC), out.shape
    return out


class StopEmit(Exception):
    pass


def build(dbg=None, stop_after=None, do_sample=True, n_mt=NMT, cut=None):
    nc = bass.Bass("TRN2", target_bir_lowering=False)
    kb = KB(nc)
    dbg = dbg or {}

    def din(name, shape, dt=F32):
        return kb.dram(name, shape, dt, "ExternalInput")

    def dout(name, shape):
        return kb.dram(name, shape, F32, "ExternalOutput")

    xp = din("xp", [SEQ, D])
    xs = din("xs", [128, D])
    sshift = din("sshift", [2, NSEQ, D])
    swkv = din("swkv", [2, NSEQ, NH, HD, HD])
    spool = din("spool", [2, NSEQ, PBUF, C])
    vecs_d = din("vecs", [128, NVEC])
    fnorm_d = din("final_norm", [D])
    rnorm_d = din("rwkv_norm", [2, D])
    cd = {k: din("c_" + k, list(shp)) for k, shp in CONST_SHAPES.items()}
    W = {}
    for nm, shp in [("rwkv_w_r", [2, D, C]), ("rwkv_w_k", [2, D, C]), ("rwkv_w_v", [2, D, C]), ("rwkv_w_z", [2, D, C]),
                    ("rwkv_w1", [2, D, W_LORA]), ("rwkv_w2", [2, W_LORA, C]), ("rwkv_a1", [2, D, A_LORA]),
                    ("rwkv_a2", [2, A_LORA, C]), ("rwkv_v1", [1, D, V_LORA]), ("rwkv_v2", [1, V_LORA, C]),
                    ("rwkv_w_o", [2, C, D]), ("pool_w_in", [2, D, 2 * C]), ("pool_w_grp", [2, 4, 512, 512]),
                    ("pool_w_o", [2, C, D])]:
        W[nm] = din(nm, shp)

    y_p = dout("y_p", [SEQ, D])
    y_s = dout("y_s", [128, D])
    shp_o = dout("shp", [2, D])
    wkvp_o = dout("wkvp", [2, NH, HD, HD])
    poolp_o = dout("poolp", [2, PBUF, C])
    shs_o = dout("shs", [2, NSEQ, D])
    wkvs_o = dout("wkvs", [2, NSEQ, NH, HD, HD])
    pools_u = dout("pools_u", [2, 128, C])
    pools_old = dout("pools_old", [2, NSEQ, 7, C])
    dbg_o = {k: dout("dbg_" + k, shp) for k, shp in dbg.items()}

    PID = {}
    npiece = 0

    def newp(key):
        nonlocal npiece
        PID[key] = npiece
        npiece += 1

    for j in range(2):
        for g in range(NG):
            for nm in ("r", "k", "v", "z"):
                newp(("rw", j, nm, g))
            newp(("l2", j, g))
        newp(("l1", j))
        for q in range(4):
            for hf in range(2):
                newp(("rwo", j, q, hf))
        for g in range(4):
            for hf in range(2):
                newp(("pu", j, g, hf))
                newp(("pz", j, g, hf))
            newp(("pg", j, g))
        for q in range(4):
            for hf in range(2):
                newp(("pwo", j, q, hf))
    scr = kb.dram("wscr", [npiece, 128, 2048], BF16, "Internal")
    scr_res = [Res() for _ in range(npiece)]

    def scrv(key):
        i = PID[key]
        return V(scr.h[i], scr_res[i])

    def prepass():
        def cast(key, dst_view, src_view):
            i = PID[key]
            kb.dma("pool", V(dst_view, scr_res[i]), src_view)

        for j in range(2):
            for g in range(NG):
                for nm in ("r", "k", "v", "z"):
                    src = W["rwkv_w_" + nm].h[j].rearrange("(c p) n -> p c n", p=128)[:, :, g * 256:(g + 1) * 256]
                    dst = scr.h[PID[("rw", j, nm, g)]].rearrange("p (c n) -> p c n", n=256)
                    cast(("rw", j, nm, g), dst, V(src, W["rwkv_w_" + nm].res))
                dst = scr.h[PID[("l2", j, g)]].rearrange("p (c n) -> p c n", n=256)
                cast(("l2", j, g), dst[0:W_LORA, 0, :], V(W["rwkv_w2"].h[j][:, g * 256:(g + 1) * 256], W["rwkv_w2"].res))
                cast(("l2", j, g), dst[0:A_LORA, 1, :], V(W["rwkv_a2"].h[j][:, g * 256:(g + 1) * 256], W["rwkv_a2"].res))
                if j == 1:
                    cast(("l2", j, g), dst[0:V_LORA, 2, :], V(W["rwkv_v2"].h[0][:, g * 256:(g + 1) * 256], W["rwkv_v2"].res))
            dst = scr.h[PID[("l1", j)]].rearrange("p (c n) -> p c n", n=256)
            cast(("l1", j), dst[:, :, 0:96], V(W["rwkv_w1"].h[j].rearrange("(c p) n -> p c n", p=128), W["rwkv_w1"].res))
            cast(("l1", j), dst[:, :, 96:192], V(W["rwkv_a1"].h[j].rearrange("(c p) n -> p c n", p=128), W["rwkv_a1"].res))
            if j == 1:
                cast(("l1", j), dst[:, :, 192:256], V(W["rwkv_v1"].h[0].rearrange("(c p) n -> p c n", p=128), W["rwkv_v1"].res))
            for q in range(4):
                for hf in range(2):
                    for (pk, wn) in ((("rwo", j, q, hf), "rwkv_w_o"), (("pwo", j, q, hf), "pool_w_o")):
                        src = W[wn].h[j].rearrange("(c p) n -> p c n", p=128)[:, hf * 8:(hf + 1) * 8, q * 256:(q + 1) * 256]
                        dst = scr.h[PID[pk]].rearrange("p (c n) -> p c n", n=256)
                        cast(pk, dst, V(src, W[wn].res))
            for g in range(4):
                for hf in range(2):
                    c0 = g * 512 + hf * 256
                    src = W["pool_w_in"].h[j].rearrange("(c p) n -> p c n", p=128)
                    dst = scr.h[PID[("pu", j, g, hf)]].rearrange("p (c n) -> p c n", n=256)
                    cast(("pu", j, g, hf), dst, V(src[:, :, c0:c0 + 256], W["pool_w_in"].res))
                    dst = scr.h[PID[("pz", j, g, hf)]].rearrange("p (c n) -> p c n", n=256)
                    cast(("pz", j, g, hf), dst, V(src[:, :, C + c0:C + c0 + 256], W["pool_w_in"].res))
                src = W["pool_w_grp"].h[j, g].rearrange("(c p) n -> p c n", p=128)
                dst = scr.h[PID[("pg", j, g)]].rearrange("p (c n) -> p c n", n=512)
                cast(("pg", j, g), dst, V(src, W["pool_w_grp"].res))

    ident = kb.sb([128, 128], F32, "ident")
    identb = kb.sb([128, 128], BF16, "identb")
    blk2 = kb.sb([128, 128], F32, "blk2")
    msk = {k: kb.sb([128, 128], BF16, k) for k in ("ms_p", "mi_p", "mst_p", "ms_s", "mi_s", "mst_s")}
    msi4 = {k: kb.sb([128, 4, 128], BF16, "msi4" + k) for k in ("_p", "_s")}
    nf = {k: kb.sb([128, TT], F32, k) for k in ("nf_p", "nf_s")}
    selb = kb.sb([128, NSEQ, 128], BF16, "selb")
    seltb = kb.sb([128, NSEQ], BF16, "seltb")
    bands = {k: kb.sb([128, 4, 128], F32, k) for k in ("bc", "bf", "bsc")}
    bpc = kb.sb([64, 4, 128], F32, "bpc")
    bprev_full = kb.sb([128, 4, 128], F32, "bprev")
    bprev64 = kb.view(bprev_full.h[64:128, :, :], bprev_full.res)
    bsp = kb.sb([128, 8, 128], F32, "bsp")
    vecs = kb.sb([128, NVEC], F32, "vecs")
    omka = kb.sb([128, 32], F32, "omka")
    neghalf = kb.sb([128, 8], F32, "neghalf")
    pbs = kb.sb([128, 32], F32, "pbs")
    hbias = kb.sb([128, 80], F32, "hbias")
    stage = kb.sb([128, NSEQ * 128], F32, "stage")

    def vec(name, idx):
        o = VEC_OFF[name] + idx
        return vecs[:, o:o + 1]

    def setup():
        for k in ("ident", "blk2"):
            kb.dma("sp", {"ident": ident, "blk2": blk2}[k].all(), cd[k].all())
        for k in bands:
            kb.dma("sp", bands[k].all(), cd[k].all())
        kb.dma("sp", bsp.all(), cd["bsp"].all())
        kb.dma("sp", bpc.all(), cd["bpc"].all())
        kb.dma("sp", bprev_full.all(), cd["bprev64"].all())
        for k in nf:
            kb.dma("sp", nf[k].all(), cd[k].all())
        kb.dma("sp", vecs.all(), vecs_d.all())
        kb.cp(identb.all(), ident.all())
        for k in msk:
            st = stage[:, 0:128]
            kb.dma("sp", st, cd[k].all())
            kb.cp(msk[k].all(), st)
        for sfx_ in ("_p", "_s"):
            for q_, nm_ in enumerate(("ms", "mi", "ms", "mi")):
                kb.cp(msi4[sfx_][:, q_, :], msk[nm_ + sfx_].all())
        st = stage[:, 0:NSEQ * 128]
        kb.dma("sp", st, cd["sel"].all().re("p b t -> p (b t)"))
        kb.cp(selb.all().re("p b t -> p (b t)"), st)
        st = stage[:, 0:NSEQ]
        kb.dma("sp", st, cd["selt"].all())
        kb.cp(seltb.all(), st)
        for o_, nm_, n_ in ((0, "w0", 32), (32, "a0", 32), (64, "v0", 16)):
            kb.ts(hbias[:, o_:o_ + n_], vecs[:, VEC_OFF[nm_]:VEC_OFF[nm_] + n_], 0.5, ALU.mult)
        kb.op("dve", lambda e: e.memset(neghalf.all().ap, -0.5), [], [neghalf])
        ka = vecs[:, VEC_OFF["k_a"]:VEC_OFF["k_a"] + 32]
        kb.ts(omka.all(), ka, -1.0, ALU.mult, 1.0, ALU.add)
        kb.tt(pbs.all(), vecs[:, VEC_OFF["pb"]:VEC_OFF["pb"] + 32], vecs[:, VEC_OFF["pscale"]:VEC_OFF["pscale"] + 32], ALU.mult)

    x_tok = kb.sb([128, NSUB, D], F32, "x_tok")
    xnT = kb.sb([128, NCH, TT + 1], F32, "xnT")
    ybuf = kb.sb([128, NHP * TT], BF16, "ybuf")
    yzT = kb.view(ybuf.h[:].rearrange("p (c t) -> p c t", t=TT), ybuf.res)
    dx = kb.view(ybuf.h[:].bitcast(F32).rearrange("p (c t) -> p c t", t=TT), ybuf.res)
    xm_ = [kb.sb([128, NCH, TT], BF16, f"xm{i}") for i in range(5)]
    MR, MW, MK, MV, MA, MG = range(6)
    xm = {MR: xm_[0], MK: xm_[1], MV: xm_[2], MG: xm_[3], MW: xm_[4], MA: xm_[4]}
    mids = kb.sb([128, 3, TT], BF16, "mids")
    big = kb.view(stage.h[:, 0:D], stage.res)
    vfirst = kb.sb([128, NHP, TT], BF16, "vfirst")
    shiftc = [kb.sb([128, NCH, 1], F32, f"shiftc{j}") for j in range(2)]
    ST = [kb.sb([128, NHP, HD], F32, f"ST{j}") for j in range(2)]
    STb = [kb.sb([128, NHP, HD], BF16, f"STb{j}") for j in range(2)]
    cc_ = [kb.sb([64, 1024], F32, f"cc{j}") for j in range(2)]
    small = kb.sb([128, 64], F32, "small")

    NSLOT = 5
    wslots = [kb.sb([128, 2048], BF16, f"wslot{i}") for i in range(NSLOT)]
    wnext = [0]

    def wload(key, parts=None):
        s = wslots[wnext[0] % NSLOT]
        wnext[0] += 1
        if parts is None:
            kb.dma("sp", s.all(), scrv(key))
        else:
            src = scrv(key)
            for f in parts:
                kb.dma("sp", V(f(s.h[:]), s.res), V(f(src.ap), src.res))
        return s

    banks = [kb.ps(f"bank{i}") for i in range(8)]
    PJ = [banks[0], banks[1]]
    MMB = [banks[3], banks[4], banks[5], banks[6]]
    NEB = MMB
    CHB = [banks[7], banks[2]]
    PQ = [banks[0], banks[1], banks[3], banks[4], banks[5], banks[6]]
    rr = {"pj": 0, "mm": 0, "ne": 0, "ch": 0, "pq": 0}

    def bank(kind):
        pool = {"pj": PJ, "mm": MMB, "ne": NEB, "ch": CHB, "pq": PQ}[kind]
        if kind == "ne":
            kind = "mm"
        b = pool[rr[kind] % len(pool)]
        rr[kind] += 1
        return b

    class Pool_:
        def __init__(self, n, shape, dt, name):
            self.t = [kb.sb(shape, dt, f"{name}{i}") for i in range(n)]
            self.i = 0
            self.nbase = n
            self.n_active = n

        def get(self):
            t = self.t[self.i % self.n_active]
            self.i += 1
            return t

    def dbg_dump(name, view):
        if name in dbg_o:
            kb.dma("pool", dbg_o[name].all(), view, is_out=True)

    def rmsnorm_to_xnT(nsub, gname, gidx, out_bf=None, keep_xs=None):
        for st in range(nsub):
            ss = small[:, st:st + 1]
            kb.act(big.all(), x_tok[:, st, :], AF.Square, accum=ss)
            rs = small[:, 8 + st:9 + st]
            kb.ts(rs, ss, 1.0 / D, ALU.mult, NORM_EPS, ALU.add)
            kb.tt(rs, rs, neghalf[:, 0:1], ALU.pow, eng="pool")
            kb.act(big.all(), x_tok[:, st, :], AF.Copy, scale=rs)
            for half in range(2):
                b = bank("pj")
                kb.pe([kb.tr(b[:, q * 128:(q + 1) * 128], big[:, (half * 4 + q) * 128:(half * 4 + q + 1) * 128], ident.all()) for q in range(4)],
                      [big, ident], [b])
                for q in range(4):
                    c = half * 4 + q
                    dstv = xnT[:, c, 1 + st * 128:1 + (st + 1) * 128]
                    kb.ts(dstv, b[:, q * 128:(q + 1) * 128], vec(gname, gidx * 8 + c), ALU.mult)
                    if out_bf is not None:
                        kb.cp(out_bf[:, c, st * 128:(st + 1) * 128], dstv, eng="act")
            if keep_xs is not None:
                keep_xs(st)

    def out_proj(nsub, pk, j):
        for q in range(4):
            wp = [wload((pk, j, q, hf)) for hf in range(2)]
            for st in range(nsub):
                b = bank("pq")
                fns = []
                for c in range(NHP):
                    w = wp[c // 8].all().re("p (c n) -> p c n", n=256)
                    fns.append(kb.mm(b[:, 0:256], yzT[:, c, st * 128:(st + 1) * 128], w[:, c % 8, :], start=(c == 0), stop=(c == NHP - 1)))
                kb.pe(fns, [yzT, wp[0], wp[1]], [b])
                xv = x_tok[:, st, q * 256:(q + 1) * 256]
                kb.tt(xv, b[:, 0:256], xv, ALU.add)

    f32p = Pool_(22, [128, TT], F32, "f")
    bon_p = Pool_(2, [128, TT], F32, "bon")
    ARp = Pool_(2, [128, 2, TT], BF16, "ar")
    bkp = Pool_(4, [128, TT], BF16, "bk")
    tokp = Pool_(2, [128, NSUB, 3, 128], BF16, "tok")
    szp = Pool_(2, [128, TT], BF16, "sz")
    t3_p = Pool_(6, [128, 3, 128], BF16, "t3")
    m4_p = Pool_(6, [128, 4, 128], BF16, "m4")
    mt0_p = Pool_(4, [128, 128], BF16, "mt0")
    tfin_p = Pool_(6, [128, 128], BF16, "tfin")
    xu_p = Pool_(6, [128, 128], BF16, "xu")
    ytok_p = Pool_(4, [128, NSUB, 128], F32, "ytok")
    wc_p = Pool_(2, [128, NSEQ], F32, "wc")
    ubuf = kb.sb([128, 2 * NSUB * 512], F32, "ubuf")
    ures = [Res(), Res()]
    u_tiles = [kb.view(ubuf.h[:, i * NSUB * 512:(i + 1) * NSUB * 512].rearrange("p (s n) -> p s n", n=512), ures[i]) for i in range(2)]
    arx = kb.view(ubuf.h[:].bitcast(BF16).rearrange("p (q b t) -> p q b t", q=2, b=NSEQ), tuple(ures))
    pbuf = kb.sb([128, 4 * 4 * TT], BF16, "pbuf")
    pres = [Res() for _ in range(4)]
    pviews = [kb.view(pbuf.h[:, i * 4 * TT:(i + 1) * 4 * TT].rearrange("p (c t) -> p c t", t=TT), pres[i]) for i in range(4)]
    utx = kb.view(pbuf.h[:, 0:NSEQ * 128].rearrange("p (b f) -> p b f", f=128), (pres[0], pres[1]))
    vx = kb.view(pbuf.h[:, NSEQ * 128:2 * NSEQ * 128].rearrange("p (b f) -> p b f", f=128), (pres[2], pres[3]))
    pb_ = pbuf.h
    allp = tuple(pres)
    o_ = 0
    for i_ in range(2):
        ARp.t.append(kb.view(pb_[:, o_:o_ + 2 * TT].rearrange("p (q t) -> p q t", q=2), allp))
        o_ += 2 * TT
    for i_ in range(4):
        bkp.t.append(kb.view(pb_[:, o_:o_ + TT], allp))
        o_ += TT
    for i_ in range(2):
        tokp.t.append(kb.view(pb_[:, o_:o_ + NSUB * 384].rearrange("p (s q f) -> p s q f", q=3, f=128), allp))
        o_ += NSUB * 384
    for i_ in range(2):
        szp.t.append(kb.view(pb_[:, o_:o_ + TT], allp))
        o_ += TT
    assert o_ <= 4 * 4 * TT, o_
    ub_ = ubuf.h
    allu = tuple(ures)
    for i_ in range(2):
        bon_p.t.append(kb.view(ub_[:, i_ * TT:(i_ + 1) * TT], allu))
        wc_p.t.append(kb.view(ub_[:, 2 * TT + i_ * NSEQ:2 * TT + (i_ + 1) * NSEQ], allu))
    sbuf1 = kb.sb([128, 1024], F32, "sbuf1")
    spst = kb.view(sbuf1.h[:].rearrange("p (h n) -> p h n", n=512), sbuf1.res)
    STs = kb.view(sbuf1.h[:].rearrange("p (b i) -> p b i", i=HD), sbuf1.res)
    sbuf2 = kb.sb([128, NCH * TT], BF16, "sbuf2")
    xnb = kb.view(sbuf2.h[:].rearrange("p (c t) -> p c t", t=TT), sbuf2.res)
    STsb = kb.view(sbuf2.h[:, 0:NSEQ * HD].rearrange("p (b i) -> p b i", i=HD), sbuf2.res)
    s0stage = kb.view(stage.h[0:HD, :].rearrange("p (b f) -> p b f", f=128), stage.res)

    def rwkv_layer(j, sample, mt, last):
        tt = 128 if sample else TT
        nsub = 1 if sample else NSUB
        nchunk = nsub
        nlev = 3 if sample else 7
        sfx = "_s" if sample else "_p"
        MS, MI, MST = msk["ms" + sfx], msk["mi" + sfx], msk["mst" + sfx]
        MSI4 = msi4[sfx]
        NFm = nf["nf" + sfx]

        def keep(st):
            if sample:
                gb = stage[:, D:2 * D]
                kb.dma("sp", gb, V(rnorm_d.h[j].partition_broadcast(128), rnorm_d.res))
                kb.tt(gb, big.all(), gb, ALU.mult)
                kb.dma("pool", shs_o[j, :, :], V(stage.h[7:128:8, D:2 * D], stage.res), is_out=True)
            elif last and st == nsub - 1:
                gb = stage[:, D:2 * D]
                kb.dma("sp", gb, V(rnorm_d.h[j].partition_broadcast(128), rnorm_d.res))
                kb.tt(gb, big.all(), gb, ALU.mult)
                kb.dma("pool", shp_o[j:j + 1, :], stage[127:128, D:2 * D], is_out=True)

        if not sample:
            if mt == 0:
                kb.op("dve", lambda e: e.memset(shiftc[j].all().ap, 0.0), [], [shiftc[j]])
            kb.cp(xnT[:, :, 0:1], shiftc[j].all())
        rmsnorm_to_xnT(nsub, "rnorm", j, keep_xs=keep)
        if not sample:
            kb.tt(dx[:, :, 0:tt], xnT[:, :, 0:tt], xnT[:, :, 1:tt + 1], ALU.subtract)
            kb.cp(shiftc[j].all(), xnT[:, :, tt:tt + 1])
        else:
            sst = stage[0:NSEQ, 0:D]
            kb.dma("sp", sst, sshift[j, :, :])
            b = bank("pj")
            kb.pe([kb.tr(b[:, c * NSEQ:(c + 1) * NSEQ], stage[0:NSEQ, c * 128:(c + 1) * 128], ident[0:NSEQ, 0:NSEQ]) for c in range(NCH)],
                  [stage, ident], [b])
            shT = f32p.get()
            kb.cp(shT[:, 0:NCH * NSEQ], b[:, 0:NCH * NSEQ])
            xn4 = xnT[:, :, 1:129].re("p c (b t) -> p c b t", t=TS)
            dx4 = dx[:, :, 0:128].re("p c (b t) -> p c b t", t=TS)
            sh3 = shT[:, 0:NCH * NSEQ].re("p (c b) -> p c b", b=NSEQ)
            for c in range(NCH):
                kb.tt(dx4[:, c, :, 1:TS], xn4[:, c, :, 0:TS - 1], xn4[:, c, :, 1:TS], ALU.subtract)
                kb.tt(dx4[:, c, :, 0], sh3[:, c, :], xn4[:, c, :, 0], ALU.subtract)
        if cut == 1:
            raise StopEmit()
        def mix(mi):
            for c in range(NCH):
                kb.stt(xm[mi][:, c, 0:tt], dx[:, c, 0:tt], vec("mu", (j * 6 + mi) * 8 + c), xnT[:, c, 1:tt + 1], ALU.mult, ALU.add)

        l1n = 192 if j == 0 else 256
        l1 = wload(("l1", j), parts=[lambda a: a.rearrange("p (c n) -> p c n", n=256)[:, :, 0:l1n]]).all().re("p (c n) -> p c n", n=256)

        def lora_mid(li, c0, ncol, mix_id, fn):
            b = bank("pj")
            kb.pe([kb.mm(b[0:ncol, 0:tt], l1[:, c, c0:c0 + ncol], xm[mix_id][:, c, 0:tt], start=(c == 0), stop=(c == NCH - 1)) for c in range(NCH)],
                  [l1, xm[mix_id]], [b])
            kb.act(mids[0:ncol, li, 0:tt], b[0:ncol, 0:tt], fn)

        mix(MW)
        lora_mid(0, 0, W_LORA, MW, AF.Tanh)
        mix(MA)
        lora_mid(1, 96, A_LORA, MA, AF.Copy)
        for mi in (MR, MK, MV, MG):
            mix(mi)
        if j == 1:
            lora_mid(2, 192, V_LORA, MV, AF.Copy)

        if cut == 2:
            raise StopEmit()
        states = {}

        def A_gen(g):
            wr, wk, wv, wz = (wload(("rw", j, nm, g)).all().re("p (c n) -> p c n", n=256) for nm in ("r", "k", "v", "z"))
            l2parts = [lambda a: a[0:W_LORA, 0:512]] + ([lambda a: a[0:V_LORA, 512:768]] if j == 1 else [])
            l2 = wload(("l2", j, g), parts=l2parts).all().re("p (c n) -> p c n", n=256)
            hp_state = [None, None]

            def prep_gen(hl):
                hp = 2 * g + hl
                cs0 = hl * 128

                def proj(w, mix):
                    b = bank("pj")
                    kb.pe([kb.mm(b[:, 0:tt], w[:, c, cs0:cs0 + 128], xm[mix][:, c, 0:tt], start=(c == 0), stop=(c == NCH - 1)) for c in range(NCH)],
                          [w, xm[mix]], [b])
                    return b

                def lproj(li, rank):
                    b = bank("pj")
                    kb.pe([kb.mm(b[:, 0:tt], l2[0:rank, li, cs0:cs0 + 128], mids[0:rank, li, 0:tt])], [l2, mids], [b])
                    return b

                S_t, a_t, r_t, kk_t, kh_t, v_t, q_t, tA, tB, b_t, cs_t = (f32p.get() for _ in range(11))

                def sigm(dst, src, hb):
                    kb.act(dst[:, 0:tt], src[:, 0:tt], AF.Tanh, bias=hb, scale=0.5)
                    kb.ts(dst[:, 0:tt], dst[:, 0:tt], 0.5, ALU.mult, 0.5, ALU.add)
                b = lproj(0, W_LORA)
                sigm(S_t, b, hbias[:, j * 16 + hp:j * 16 + hp + 1])
                b = lproj(1, A_LORA)
                sigm(a_t, b, hbias[:, 32 + j * 16 + hp:32 + j * 16 + hp + 1])
                yield "sub"
                b = proj(wr, MR)
                kb.cp(r_t[:, 0:tt], b[:, 0:tt], eng="act")
                yield "sub"
                b = proj(wk, MK)
                kb.ts(kk_t[:, 0:tt], b[:, 0:tt], vec("k_k", j * 16 + hp), ALU.mult)
                kb.ts(tA[:, 0:tt], a_t[:, 0:tt], vec("k_a", j * 16 + hp), ALU.mult, omka[:, j * 16 + hp:j * 16 + hp + 1], ALU.add)
                kb.tt(kh_t[:, 0:tt], b[:, 0:tt], tA[:, 0:tt], ALU.mult)
                yield "sub"
                b = proj(wv, MV)
                if j == 0:
                    kb.cp(v_t[:, 0:tt], b[:, 0:tt], eng="act")
                    kb.cp(vfirst[:, hp, 0:tt], v_t[:, 0:tt])
                else:
                    bg = lproj(2, V_LORA)
                    sigm(tA, bg, hbias[:, 64 + hp:64 + hp + 1])
                    kb.tt(tB[:, 0:tt], vfirst[:, hp, 0:tt], b[:, 0:tt], ALU.subtract)
                    kb.tt(tB[:, 0:tt], tB[:, 0:tt], tA[:, 0:tt], ALU.mult)
                    kb.tt(v_t[:, 0:tt], b[:, 0:tt], tB[:, 0:tt], ALU.add)
                yield "sub"
                sz_t = szp.get()
                b = proj(wz, MG)
                kb.act(tA[:, 0:tt], b[:, 0:tt], AF.Tanh, scale=0.5)
                kb.stt(sz_t[:, 0:tt], tA[:, 0:tt], 1.0, b[:, 0:tt], ALU.add, ALU.mult)
                yield "sub"
                bon_t = bon_p.get()
                kb.stt(q_t[:, 0:tt], r_t[:, 0:tt], vec("r_k", j * 16 + hp), kh_t[:, 0:tt], ALU.mult, ALU.mult)
                b = bank("pj")
                kb.pe([kb.mm(b[:, 0:tt], blk2.all(), q_t[:, 0:tt])], [blk2, q_t], [b])
                kb.tt(bon_t[:, 0:tt], b[:, 0:tt], v_t[:, 0:tt], ALU.mult)
                yield "sub"
                kb.act(q_t[:, 0:tt], kk_t[:, 0:tt], AF.Square)
                kb.op("dve", lambda e: e.tensor_tensor_scan(out=cs_t[:, 0:tt].ap, data0=NFm[:, 0:tt].ap, data1=S_t[:, 0:tt].ap,
                                                            initial=0.0, op0=ALU.mult, op1=ALU.add), [NFm, S_t], [cs_t])
                yield "stage"
                bss = bank("pj")
                kb.pe([kb.mm(bss[:, 0:tt], blk2.all(), q_t[:, 0:tt])], [blk2, q_t], [bss])
                kb.act(q_t[:, 0:tt], bss[:, 0:tt], AF.Sqrt)
                yield "stage"
                n_t = q_t
                kb.ts(n_t[:, 0:tt], n_t[:, 0:tt], 1e-12, ALU.max)
                kb.op("dve", lambda e: e.reciprocal(out=n_t[:, 0:tt].ap, in_=n_t[:, 0:tt].ap), [n_t], [n_t])
                kb.tt(kk_t[:, 0:tt], kk_t[:, 0:tt], n_t[:, 0:tt], ALU.mult)
                kb.tt(b_t[:, 0:tt], kk_t[:, 0:tt], a_t[:, 0:tt], ALU.mult, eng="pool")
                e_t, e2_t = tA, tB
                ar = ARp.get()
                kb.act(e_t[:, 0:tt], cs_t[:, 0:tt], AF.Exp, scale=-C0)
                kb.tt(ar[:, 1, 0:tt], r_t[:, 0:tt], e_t[:, 0:tt], ALU.mult, eng="pool")
                kb.tt(e2_t[:, 0:tt], cs_t[:, 0:tt], S_t[:, 0:tt], ALU.subtract)
                kb.act(e2_t[:, 0:tt], e2_t[:, 0:tt], AF.Exp, scale=-C0)
                kb.stt(ar[:, 0, 0:tt], kk_t[:, 0:tt], -1.0, e2_t[:, 0:tt], ALU.mult, ALU.mult)
                yield "sub"
                kT, bT = bkp.get(), bkp.get()
                kb.act(e_t[:, 0:tt], cs_t[:, 0:tt], AF.Exp, scale=C0)
                kb.tt(kT[:, 0:tt], kh_t[:, 0:tt], e_t[:, 0:tt], ALU.mult, eng="pool")
                kb.tt(bT[:, 0:tt], b_t[:, 0:tt], e_t[:, 0:tt], ALU.mult, eng="pool")
                yield "sub"
                wc = wc_p.get()
                if sample:
                    cs3 = cs_t[:, 0:128].re("p (b t) -> p b t", t=TS)
                    kb.tt(e2_t[:, 0:128].re("p (b t) -> p b t", t=TS), cs3, cs3[:, :, TS - 1:TS].bc([128, NSEQ, TS]), ALU.subtract)
                    kb.act(wc[:, 0:NSEQ], cs3[:, :, TS - 1], AF.Exp, scale=-C0)
                else:
                    for c in range(nchunk):
                        kb.ts(e2_t[:, c * 128:(c + 1) * 128], cs_t[:, c * 128:(c + 1) * 128], cs_t[:, c * 128 + 127:c * 128 + 128], ALU.subtract)
                        kb.act(wc[:, c:c + 1], cs_t[:, c * 128 + 127:c * 128 + 128], AF.Exp, scale=-C0)
                kb.act(e2_t[:, 0:tt], e2_t[:, 0:tt], AF.Exp, scale=C0)
                khat, bhat = S_t, e_t
                kb.tt(khat[:, 0:tt], kh_t[:, 0:tt], e2_t[:, 0:tt], ALU.mult, eng="pool")
                kb.tt(bhat[:, 0:tt], b_t[:, 0:tt], e2_t[:, 0:tt], ALU.mult, eng="pool")
                yield "sub"
                tok3 = tokp.get()
                for c in range(nchunk):
                    b = bank("pj")
                    srcs = (v_t, khat, bhat)
                    kb.pe([kb.tr(b[:, i * 128:(i + 1) * 128], srcs[i][:, c * 128:(c + 1) * 128], ident.all()) for i in range(3)],
                          [v_t, khat, bhat, ident], [b])
                    kb.cp(tok3[:, c, :, :].re("p q f -> p (q f)"), b[:, 0:384], eng="act")
                Vt = kb.view(tok3.h[:, :, 0, :], tok3.res)
                Kt = kb.view(tok3.h[:, :, 1, :], tok3.res)
                Bt = kb.view(tok3.h[:, :, 2, :], tok3.res)
                hp_state[hl] = dict(hp=hp, ar=ar, kT=kT, bT=bT, Vt=Vt, Kt=Kt, Bt=Bt, wc=wc, bon=bon_t, sz=sz_t)

            pg = [prep_gen(0), prep_gen(1)]
            for _stage in range(3):
                for gq in pg:
                    while True:
                        try:
                            tag = next(gq)
                        except StopIteration:
                            break
                        yield
                        if tag == "stage":
                            break
            states[g] = hp_state

        def B_gen(g):
            hp_state = states[g]
            def scan_gen(hs, hl):
                hp = hs["hp"]
                ar, kT, bT, Vt, Kt, Bt, wc = hs["ar"], hs["kT"], hs["bT"], hs["Vt"], hs["Kt"], hs["Bt"], hs["wc"]
                ytok = ytok_p.get()
                if sample:
                    for h in range(2):
                        src = swkv.h[j, :, 2 * hp + h, :, :].rearrange("b i j -> i b j")
                        kb.dma("sp", s0stage[:, :, h * HD:(h + 1) * HD], V(src, swkv.res))
                    for q4 in range(4):
                        b = bank("pj")
                        kb.pe([kb.tr(b[:, q * HD:(q + 1) * HD], s0stage[:, q4 * 4 + q, :], ident[0:HD, 0:HD]) for q in range(4)], [s0stage, ident], [b])
                        kb.cp(STs[:, q4 * 4:(q4 + 1) * 4, :].re("p b i -> p (b i)"), b[:, 0:4 * HD])
                    kb.cp(STsb.all(), STs.all(), eng="act")
                    for q in range(2):
                        kb.tt(arx[:, q, :, :], ar[:, q, 0:128].re("p (o t) -> p o t", o=1).bc([128, NSEQ, 128]), selb.all(), ALU.mult)
                elif mt == 0:
                    kb.op("dve", lambda e: e.memset(ST[j][:, hp, :].ap, 0.0), [], [ST[j]])
                    kb.op("dve", lambda e: e.memset(STb[j][:, hp, :].ap, 0.0), [], [STb[j]])
                tres = {}

                def tphase(c):
                    cols = slice(c * 128, (c + 1) * 128)
                    Tm, Mak, Mbr, Mkr = [], [], [], []
                    bmt = bank("pj")
                    cur = []
                    for h in range(2):
                        P = slice(64 * h, 64 * h + 64)
                        bm = bank("mm")
                        kb.pe([kb.mm(bm[:, 0:256].re("p (q t) -> p q t", q=2), bT[P, cols], ar[P, :, cols]),
                               kb.mm(bm[:, 256:512].re("p (q t) -> p q t", q=2), kT[P, cols], ar[P, :, cols])],
                              [bT, kT, ar], [bm])
                        kb.pe([kb.mm(bmt[:, h * 128:(h + 1) * 128], ar[P, 0, cols], bT[P, cols])], [ar, bT], [bmt])
                        m4 = m4_p.get()
                        mt0 = mt0_p.get()
                        kb.tt(m4.all(), bm[:, 0:512].re("p (q t) -> p q t", q=4), MSI4.all(), ALU.mult)
                        kb.tt(mt0.all(), bmt[:, h * 128:(h + 1) * 128], MST.all(), ALU.mult)
                        Mbr.append(m4[:, 1, :])
                        Mak.append(m4[:, 2, :])
                        Mkr.append(m4[:, 3, :])
                        cur.append((m4, mt0))
                        yield
                    for lev in range(nlev):
                        lastl = lev == nlev - 1
                        for h in range(2):
                            bn = bank("ne")
                            if lev == 0:
                                m4, mt0 = cur[h]
                                M0 = m4[:, 0, :]
                                kb.pe([kb.mm(bn[:, 0:128], mt0.all(), M0, start=True, stop=True),
                                       kb.mm(bn[:, 128:256], mt0.all(), identb.all(), start=True, stop=False),
                                       kb.mm(bn[:, 128:256], identb.all(), identb.all(), start=False, stop=True),
                                       kb.mm(bn[:, 256:384], M0, mt0.all(), start=True, stop=True)], [m4, mt0, identb], [bn])
                                t3n = t3_p.get()
                                kb.cp(t3n.all().re("p q t -> p (q t)"), bn[:, 0:384], eng=("act" if h % 2 == 0 else "dve"))
                                cur[h] = t3n
                                continue
                            t3 = cur[h]
                            if not lastl:
                                kb.pe([kb.mm(bn[:, 0:256].re("p (q t) -> p q t", q=2), t3[:, 2, :], t3[:, 0:2, :], start=True, stop=False),
                                       kb.mm(bn[:, 128:256], identb.all(), t3[:, 1, :], start=False, stop=True),
                                       kb.mm(bn[:, 256:384], t3[:, 0, :], t3[:, 2, :], start=True, stop=True)], [t3, identb], [bn])
                                t3n = t3_p.get()
                                kb.cp(t3n.all().re("p q t -> p (q t)"), bn[:, 0:384], eng=("act" if (h + lev) % 2 == 0 else "dve"))
                                cur[h] = t3n
                            else:
                                kb.pe([kb.mm(bn[:, 0:128], t3[:, 2, :], t3[:, 1, :], start=True, stop=False),
                                       kb.mm(bn[:, 0:128], identb.all(), t3[:, 1, :], start=False, stop=True)], [t3, identb], [bn])
                                t_T = tfin_p.get()
                                kb.cp(t_T.all(), bn[:, 0:128], eng=("act" if h == 0 else "dve"))
                                Tm.append(t_T.all())
                        yield

                    tres[c] = (cols, Tm, Mak, Mbr, Mkr)

                def chain(c):
                    cols, Tm, Mak, Mbr, Mkr = tres[c]
                    stb = STsb if sample else STb[j]
                    bx = bank("ch")
                    fns = []
                    for h in range(2):
                        P = slice(64 * h, 64 * h + 64)
                        oc = slice(64 * h, 64 * h + 64)
                        if sample:
                            for bq in range(NSEQ):
                                fns.append(kb.mm(bx[:, oc], arx[P, 0, bq, :], stb[P, bq, :], start=(bq == 0), stop=False))
                        else:
                            fns.append(kb.mm(bx[:, oc], ar[P, 0, cols], stb[P, hp, :], start=True, stop=False))
                        fns.append(kb.mm(bx[:, oc], Mak[h], Vt[:, c, oc], start=False, stop=True))
                    kb.pe(fns, [ar, arx, stb, Vt, Mak[0], Mak[1]], [bx])
                    xtb = xu_p.get()
                    kb.cp(xtb.all(), bx[:, 0:128], eng="dve")
                    yield
                    bu = bank("ch")
                    kb.pe([kb.mm(bu[:, 64 * h:64 * h + 64], Tm[h], xtb[:, 64 * h:64 * h + 64]) for h in range(2)], [Tm[0], Tm[1], xtb], [bu])
                    utb = xu_p.get()
                    kb.cp(utb.all(), bu[:, 0:128], eng="dve")
                    yield
                    by = bank("ch")
                    fns = []
                    for h in range(2):
                        P = slice(64 * h, 64 * h + 64)
                        oc = slice(64 * h, 64 * h + 64)
                        if sample:
                            for bq in range(NSEQ):
                                fns.append(kb.mm(by[:, oc], arx[P, 1, bq, :], stb[P, bq, :], start=(bq == 0), stop=False))
                        else:
                            fns.append(kb.mm(by[:, oc], ar[P, 1, cols], stb[P, hp, :], start=True, stop=False))
                        fns.append(kb.mm(by[:, oc], Mbr[h], utb[:, oc], start=False, stop=False))
                        fns.append(kb.mm(by[:, oc], Mkr[h], Vt[:, c, oc], start=False, stop=True))
                    kb.pe(fns, [ar, arx, stb, utb, Vt, Mbr[0], Mbr[1], Mkr[0], Mkr[1]], [by])
                    kb.cp(ytok[:, c, :], by[:, 0:128], eng="act")
                    yield
                    if hp == 0 and mt == 0 and j == 0 and not sample and c == 0:
                        dbg_dump("T0", Tm[0]); dbg_dump("Mak0", Mak[0]); dbg_dump("Mbr0", Mbr[0]); dbg_dump("Mkr0", Mkr[0])
                        dbg_dump("XT", xtb.all()); dbg_dump("UT", utb.all()); dbg_dump("Vt", Vt[:, 0, :])
                        dbg_dump("arT", ar.all().re("p q t -> p (q t)")); dbg_dump("kT", kT.all()); dbg_dump("bT", bT.all())
                    if cut == 5 or cut == 60 + c:
                        raise StopEmit()
                    if sample:
                        kb.tt(utx.all(), utb.all().re("p (o f) -> p o f", o=1).bc([128, NSEQ, 128]), seltb.all().re("p (b o) -> p b o", o=1).bc([128, NSEQ, 128]), ALU.mult)
                        kb.tt(vx.all(), Vt[:, 0, :].re("p (o f) -> p o f", o=1).bc([128, NSEQ, 128]), seltb.all().re("p (b o) -> p b o", o=1).bc([128, NSEQ, 128]), ALU.mult)
                        for half in range(2):
                            bs = bank("pj")
                            fns = []
                            for h in range(2):
                                P = slice(64 * h, 64 * h + 64)
                                oc = slice(64 * h, 64 * h + 64)
                                o3 = bs[P, :].re("p (b i) -> p b i", i=HD)
                                fns.append(kb.mm(o3, Bt[:, 0, oc], utx[:, half * 8:(half + 1) * 8, oc], start=True, stop=False))
                                fns.append(kb.mm(o3, Kt[:, 0, oc], vx[:, half * 8:(half + 1) * 8, oc], start=False, stop=True))
                            kb.pe(fns, [Bt, Kt, utx, vx], [bs])
                            sv = STs[:, half * 8:(half + 1) * 8, :]
                            kb.tt(sv, sv, wc[:, half * 8:(half + 1) * 8].re("p (b o) -> p b o", o=1).bc([128, 8, HD]), ALU.mult)
                            kb.tt(sv, sv, bs[:, :].re("p (b i) -> p b i", i=HD), ALU.add)
                    else:
                        bs = bank("ch")
                        fns = []
                        for h in range(2):
                            P = slice(64 * h, 64 * h + 64)
                            oc = slice(64 * h, 64 * h + 64)
                            fns.append(kb.mm(bs[P, 0:HD], Bt[:, c, oc], utb[:, oc], start=True, stop=False))
                            fns.append(kb.mm(bs[P, 0:HD], Kt[:, c, oc], Vt[:, c, oc], start=False, stop=True))
                        kb.pe(fns, [Bt, Kt, utb, Vt], [bs])
                        if cut == 51:
                            raise StopEmit()
                        kb.stt(ST[j][:, hp, :], ST[j][:, hp, :], wc[:, c:c + 1], bs[:, 0:HD], ALU.mult, ALU.add)
                        if cut == 52:
                            raise StopEmit()
                        kb.cp(STb[j][:, hp, :], ST[j][:, hp, :], eng="act")
                    yield

                if nchunk > 1 and not sample and TPHASE_INTERLEAVE and cut is None:
                    tg = [tphase(c) for c in range(nchunk)]
                    while tg:
                        for t_ in list(tg):
                            try:
                                next(t_)
                                yield
                            except StopIteration:
                                tg.remove(t_)
                    for c in range(nchunk):
                        yield from chain(c)
                else:
                    for c in range(nchunk):
                        yield from tphase(c)
                        if cut == 4 or cut == 40 + c or (cut == 141 and c == 1):
                            raise StopEmit()
                        yield from chain(c)

                if cut == 6:
                    raise StopEmit()
                if sample or last:
                    nb = NSEQ if sample else 1
                    for b0 in range(0, nb, 2):
                        nq = min(2, nb - b0)
                        b = bank("pj")
                        srcv = (lambda q: STs[:, b0 + q, :]) if sample else (lambda q: ST[j][:, hp, :])
                        kb.pe([kb.tr(b[0:HD, q * 128:(q + 1) * 128], srcv(q), ident.all()) for q in range(nq)], [STs if sample else ST[j], ident], [b])
                        if sample:
                            so = f32p.get()
                        else:
                            yh_pre = ytok_p.get()
                            so = kb.view(yh_pre.h[:].rearrange("p s f -> p (s f)"), yh_pre.res)
                        kb.cp(so[0:HD, 0:nq * 128], b[0:HD, 0:nq * 128], eng="act")
                        if sample:
                            for h in range(2):
                                dst = wkvs_o.h[j, b0:b0 + nq, 2 * hp + h, :, :].rearrange("b i j -> i b j")
                                srcv = so.h[0:HD, 0:nq * 128].rearrange("i (b h j) -> i b h j", h=2, j=HD)[:, :, h, :]
                                kb.dma("pool", V(dst, wkvs_o.res), V(srcv, so.res), is_out=True)
                        else:
                            dst = wkvp_o.h[j, 2 * hp:2 * hp + 2, :, :].rearrange("h i j -> i h j")
                            kb.dma("pool", V(dst, wkvp_o.res), V(so.h[0:HD, 0:128].rearrange("i (h j) -> i h j", j=HD), so.res), is_out=True)

                if hp == 0 and mt == 0 and j == 0 and not sample:
                    dbg_dump("ytok", ytok.all().re("p s f -> p (s f)"))
                if cut == 7:
                    raise StopEmit()
                nst = nsub * 2
                y3 = ytok[:, 0:nsub, :].re("p s (h i) -> p (s h) i", i=HD)
                so_ = 4 * hl
                s1 = small[:, 16 + so_:16 + so_ + nst]
                s2 = small[:, 24 + so_:24 + so_ + nst]
                kb.op("dve", lambda e, y3=y3, s1=s1: e.tensor_reduce(out=s1.ap, in_=y3.ap, axis=AX.X, op=ALU.add), [ytok], [small])
                yield
                yh = yh_pre if (last and not sample) else ytok_p.get()
                sqv = yh.all().re("p s f -> p (s f)")
                kb.act(sqv[:, 0:nsub * 128], ytok[:, 0:nsub, :].re("p s f -> p (s f)"), AF.Square)
                kb.op("dve", lambda e, sqv=sqv, s2=s2: e.tensor_reduce(out=s2.ap, in_=sqv[:, 0:nsub * 128].re("p (a i) -> p a i", i=HD).ap, axis=AX.X, op=ALU.add), [yh], [small])
                mean = small[:, 32 + so_:32 + so_ + nst]
                var = small[:, 40 + so_:40 + so_ + nst]
                yield
                kb.ts(mean, s1, 1.0 / HD, ALU.mult)
                kb.tt(var, mean, mean, ALU.mult)
                kb.stt(var, s2, 1.0 / HD, var, ALU.mult, ALU.subtract)
                kb.ts(var, var, GN_EPS, ALU.add)
                kb.tt(var, var, neghalf[:, 0:nst], ALU.pow, eng="pool")
                yield
                yh3 = yh[:, 0:nsub, :].re("p s (h i) -> p (s h) i", i=HD)
                kb.tt(yh3, y3, mean.re("p (a o) -> p a o", o=1).bc([128, nst, HD]), ALU.subtract)
                kb.tt(yh3, yh3, var.re("p (a o) -> p a o", o=1).bc([128, nst, HD]), ALU.mult)
                yield
                b = bank("pj")
                kb.pe([kb.tr(b[:, s * 128:(s + 1) * 128], yh[:, s, :], ident.all()) for s in range(nsub)], [yh, ident], [b])
                ya = kb.view(ytok.h[:].rearrange("p s f -> p (s f)"), ytok.res)
                kb.act(ya[:, 0:tt], b[:, 0:tt], AF.Identity, scale=vec("gn_g", j * 16 + hp), bias=vec("gn_b", j * 16 + hp))
                kb.tt(ya[:, 0:tt], ya[:, 0:tt], hs["bon"][:, 0:tt], ALU.add)
                kb.stt(yzT[:, hp, 0:tt], ya[:, 0:tt], 0.5, hs["sz"][:, 0:tt], ALU.mult, ALU.mult)

            gens = [scan_gen(hs, hl) for hl, hs in enumerate(hp_state)]
            if sample:
                for gnr in gens:
                    for _ in gnr:
                        yield
                gens = []
            while gens:
                for gnr in list(gens):
                    try:
                        next(gnr)
                        yield
                    except StopIteration:
                        gens.remove(gnr)

        def run_all(gen):
            for _ in gen:
                pass

        pools_ab = (ARp, bkp, tokp, szp, bon_p, wc_p)
        overlap = (not sample) and cut is None and OVERLAP_AB
        for p_ in pools_ab:
            p_.n_active = len(p_.t) if overlap else p_.nbase
        if not overlap:
            for g in range(NG):
                run_all(A_gen(g))
                if cut == 3:
                    raise StopEmit()
                run_all(B_gen(g))
        else:
            run_all(A_gen(0))
            for g in range(NG):
                bgen = B_gen(g)
                agen = A_gen(g + 1) if g + 1 < NG else None
                done_a, done_b = agen is None, False
                while not (done_a and done_b):
                    for _ in range(AB_RATIO):
                        if not done_b:
                            try:
                                next(bgen)
                            except StopIteration:
                                done_b = True
                    if not done_a:
                        try:
                            next(agen)
                        except StopIteration:
                            done_a = True

        if mt == 0 and j == 0 and not sample:
            dbg_dump("yz", yzT.all().re("p c t -> p (c t)"))
            dbg_dump("xn", xnT.all().re("p c t -> p (c t)"))
        out_proj(nsub, "rwo", j)

    class RR:
        def __init__(self, ts):
            self.t = ts
            self.i = 0

        def get(self):
            t = self.t[self.i % len(self.t)]
            self.i += 1
            return t
    u_p = RR(u_tiles)
    szg_p = RR([pviews[0], pviews[1]])
    pT_p = RR([pviews[2], pviews[3]])

    def pool_layer(j, sample, mt, last):
        tt = 128 if sample else TT
        nsub = 1 if sample else NSUB
        rmsnorm_to_xnT(nsub, "pnorm", j, out_bf=xnb)
        for g in range(4):
            wu = [wload(("pu", j, g, hf)).all().re("p (c n) -> p c n", n=256) for hf in range(2)]
            wzz = [wload(("pz", j, g, hf)).all().re("p (c n) -> p c n", n=256) for hf in range(2)]
            wg = wload(("pg", j, g)).all().re("p (c n) -> p c n", n=512)
            u = u_p.get()
            for st in range(nsub):
                for hf in range(2):
                    b = bank("pq")
                    kb.pe([kb.mm(b[:, 0:256], xnb[:, c, st * 128:(st + 1) * 128], wu[hf][:, c, :], start=(c == 0), stop=(c == NCH - 1)) for c in range(NCH)],
                          [xnb, wu[hf]], [b])
                    kb.cp(u[:, st, hf * 256:(hf + 1) * 256], b[:, 0:256], eng="act")
            szg = szg_p.get()
            for cc in range(4):
                b = bank("pq")
                w = wzz[cc // 2]
                kb.pe([kb.mm(b[:, 0:tt], w[:, c, (cc % 2) * 128:(cc % 2) * 128 + 128], xnb[:, c, 0:tt], start=(c == 0), stop=(c == NCH - 1)) for c in range(NCH)],
                      [xnb, w], [b])
                th_ = f32p.get()
                kb.act(th_[:, 0:tt], b[:, 0:tt], AF.Tanh, scale=0.5)
                kb.stt(szg[:, cc, 0:tt], th_[:, 0:tt], 1.0, b[:, 0:tt], ALU.add, ALU.mult)
            if sample:
                for hf in range(2):
                    src = spool.h[j, hf * 8:(hf + 1) * 8, :, g * 512:(g + 1) * 512].rearrange("b r n -> (b r) n")
                    kb.dma("sp", spst[0:8 * PBUF, hf, :], V(src, spool.res))
            pT = pT_p.get()
            pb0 = 32 * (g % 2)
            for cc in range(4):
                b = bank("pq")
                fns = []
                rd = [u, cc_[j], spst]
                for st in range(nsub):
                    o = b[:, st * 128:(st + 1) * 128]
                    uc = u[:, st, cc * 128:(cc + 1) * 128]
                    if sample:
                        fns.append(kb.mm(o, uc, bands["bsc"][:, g, :], start=True, stop=False))
                        fns.append(kb.mm(o, spst[0:8 * PBUF, 0, cc * 128:(cc + 1) * 128], bsp[0:8 * PBUF, g * 2 + 0, :], start=False, stop=False))
                        fns.append(kb.mm(o, spst[0:8 * PBUF, 1, cc * 128:(cc + 1) * 128], bsp[0:8 * PBUF, g * 2 + 1, :], start=False, stop=True))
                    elif mt == 0 and st == 0:
                        fns.append(kb.mm(o, uc, bands["bf"][:, g, :], start=True, stop=True))
                    else:
                        fns.append(kb.mm(o, uc, bands["bc"][:, g, :], start=True, stop=False))
                        if st > 0:
                            prev = u[64:128, st - 1, cc * 128:(cc + 1) * 128]
                            fns.append(kb.mm(o, prev, bprev64[:, g, :], start=False, stop=True))
                        else:
                            prev = cc_[j][pb0:pb0 + 16, (g // 2) * 512 + cc * 128:(g // 2) * 512 + (cc + 1) * 128]
                            fns.append(kb.mm(o, prev, bpc[pb0:pb0 + 16, g, :], start=False, stop=True))
                kb.pe(fns, rd + [bands["bsc"], bands["bf"], bands["bc"], bpc, bprev64, bsp], [b])
                kb.cp(pT[:, cc, 0:tt], b[:, 0:tt], eng="act")
            if sample:
                kb.dma("pool", pools_u[j, :, g * 512:(g + 1) * 512], u[:, 0, :], is_out=True)
                if g == 0:
                    kb.dma("pool", pools_old[j], V(spool.h[j, :, 8:15, :], spool.res), is_out=True)
            else:
                if last:
                    kb.dma("pool", poolp_o[j, :, g * 512:(g + 1) * 512], u[128 - PBUF:128, nsub - 1, :], is_out=True)
                else:
                    kb.dma("sp", cc_[j][pb0:pb0 + 16, (g // 2) * 512:(g // 2 + 1) * 512], u[112:128, nsub - 1, :])
            for oc in range(4):
                b = bank("pq")
                kb.pe([kb.mm(b[:, 0:tt], wg[:, ic, oc * 128:(oc + 1) * 128], pT[:, ic, 0:tt], start=(ic == 0), stop=(ic == 3)) for ic in range(4)],
                      [wg, pT], [b])
                ya = f32p.get()
                kb.act(ya[:, 0:tt], b[:, 0:tt], AF.Identity, scale=vec("pscale", j * 16 + g * 4 + oc), bias=pbs[:, j * 16 + g * 4 + oc:j * 16 + g * 4 + oc + 1])
                kb.stt(yzT[:, g * 4 + oc, 0:tt], ya[:, 0:tt], 0.5, szg[:, oc, 0:tt], ALU.mult, ALU.mult)
        out_proj(nsub, "pwo", j)


    def final(sample, mt):
        nsub = 1 if sample else NSUB
        for st in range(nsub):
            ss = small[:, st:st + 1]
            kb.act(big.all(), x_tok[:, st, :], AF.Square, accum=ss)
            rs = small[:, 8 + st:9 + st]
            kb.ts(rs, ss, 1.0 / D, ALU.mult, NORM_EPS, ALU.add)
            kb.tt(rs, rs, neghalf[:, 0:1], ALU.pow, eng="pool")
            yo = stage[:, D:2 * D]
            kb.dma("sp", big.all(), V(fnorm_d.h.partition_broadcast(128), fnorm_d.res))
            kb.stt(yo, x_tok[:, st, :], rs, big.all(), ALU.mult, ALU.mult)
            if sample:
                kb.dma("pool", y_s.all(), yo, is_out=True)
            else:
                r0 = mt * TT + st * 128
                kb.dma("pool", y_p[r0:r0 + 128, :], yo, is_out=True)

    setup()
    prepass()
    tiles = [(False, m) for m in range(n_mt)] + ([(True, 0)] if do_sample else [])
    for (sample, mt) in tiles:
        last = (not sample) and (mt == NMT - 1)
        if sample:
            kb.dma("sp", x_tok[:, 0, :], xs.all())
        else:
            for st in range(NSUB):
                r0 = mt * TT + st * 128
                kb.dma("sp", x_tok[:, st, :], xp[r0:r0 + 128, :])
        for layer in range(4):
            if stop_after is not None and layer > stop_after:
                break
            try:
                if layer % 2 == 0:
                    rwkv_layer(layer // 2, sample, mt, last)
                else:
                    pool_layer(layer // 2, sample, mt, last)
            except StopEmit:
                break
        if "x_res" in dbg_o and not sample and mt == 0:
            kb.dma("sp", dbg_o["x_res"].all().re("(s p) d -> p s d", p=128), x_tok.all(), is_out=True)
        final(sample, mt)
    for tok in kb.out_toks:
        kb._wait("sp", tok)
    import os
    if os.environ.get("KDBG"):
        print("sem counts", kb.cnt, "dma", kb.dcnt, flush=True)
        print("sbuf remaining", nc.sbuf_bytes_remaining, flush=True)
    kb.es.close()
    return nc


_WNAMES = ["rwkv_w_r", "rwkv_w_k", "rwkv_w_v", "rwkv_w_z", "rwkv_w1", "rwkv_w2", "rwkv_a1", "rwkv_a2", "rwkv_v1",
           "rwkv_v2", "rwkv_w_o", "pool_w_in", "pool_w_grp", "pool_w_o"]


def make_in_maps(inp):
    consts = _consts()
    vecs = _pack_vecs(inp)
    shared = {"vecs": vecs, "final_norm": np.ascontiguousarray(inp["final_norm"], np.float32),
              "rwkv_norm": np.ascontiguousarray(inp["rwkv_norm"], np.float32)}
    for k, v in consts.items():
        shared["c_" + k] = np.ascontiguousarray(v, np.float32)
    for k in _WNAMES:
        shared[k] = np.ascontiguousarray(inp[k], np.float32)
    maps = []
    for c in range(NCORES):
        m = dict(shared)
        m["xp"] = np.ascontiguousarray(inp["x_prompt"][c], np.float32)
        m["xs"] = np.ascontiguousarray(inp["x_sample"][c * NSEQ:(c + 1) * NSEQ].reshape(128, D), np.float32)
        m["sshift"] = np.ascontiguousarray(inp["state_shift"][:, c * NSEQ:(c + 1) * NSEQ], np.float32)
        m["swkv"] = np.ascontiguousarray(inp["state_wkv"][:, c * NSEQ:(c + 1) * NSEQ], np.float32)
        m["spool"] = np.ascontiguousarray(inp["state_pool"][:, c * NSEQ:(c + 1) * NSEQ], np.float32)
        maps.append(m)
    return maps


def kernel(**inputs):
    inp = {k: np.asarray(v) for k, v in inputs.items()}
    nc = build()
    res = run_bass_kernel_spmd(nc, make_in_maps(inp), core_ids=list(range(NCORES))).results
    y_prompt = np.stack([r["y_p"] for r in res], 0).astype(np.float32)
    y_sample = np.concatenate([r["y_s"].reshape(NSEQ, TS, D) for r in res], 0).astype(np.float32)
    sh_p = np.stack([r["shp"] for r in res], 1).astype(np.float32)
    wkv_p = np.stack([r["wkvp"] for r in res], 1).astype(np.float32)
    pool_p = np.stack([r["poolp"] for r in res], 1).astype(np.float32)
    sh_s = np.concatenate([r["shs"] for r in res], 1).astype(np.float32)
    wkv_s = np.concatenate([r["wkvs"] for r in res], 1).astype(np.float32)
    pool_s = np.concatenate(
        [np.concatenate([r["pools_old"], r["pools_u"].reshape(2, NSEQ, TS, C)], axis=2) for r in res], 1).astype(np.float32)
    return (y_prompt, y_sample, sh_p, wkv_p, pool_p, sh_s, wkv_s, pool_s)
```
